# Optimizing a Trainium2 kernel written in Bass

```python
import jax
import jax.numpy as jnp
from jax import lax
import numpy as np

D_MODEL = 2048
BATCH = 4
SEQ = 2048
DEPTH = 1
DEC_BATCH = 128
DEC_SEQ = 8
PAST_LEN = 8192
PAGE_SIZE = 128

RW_HEADS = 12
RW_HEAD_DIM = 64
RW_WIDTH = RW_HEADS * RW_HEAD_DIM
RW_DECAY_RANK = 64
RW_A_RANK = 64
RW_GATE_RANK = 128
RW_PROJ = 3 * RW_WIDTH + RW_DECAY_RANK + RW_A_RANK + RW_GATE_RANK
RW_GN_EPS = 6.4e-4
SWA_Q_HEADS = 12
SWA_KV_HEADS = 4
SWA_GROUP = SWA_Q_HEADS // SWA_KV_HEADS
SWA_HEAD_DIM = 64
SWA_Q_WIDTH = SWA_Q_HEADS * SWA_HEAD_DIM
SWA_KV_WIDTH = SWA_KV_HEADS * SWA_HEAD_DIM
WINDOW = 128
MEM_TOKENS = 256
MEM_HEADS = 4
MEM_HEAD_DIM = 128
MEM_WIDTH = MEM_HEADS * MEM_HEAD_DIM
N_BRANCH = 3
P_IN = RW_PROJ + SWA_Q_WIDTH + 2 * SWA_KV_WIDTH + MEM_WIDTH + N_BRANCH * D_MODEL
D_FF = 5632
CONV_W = 3
NORM_EPS = 1e-6

kernel_name = 'hybrid_rwkv7_swa_sink_mem_convglu_step'


def _rmsnorm(x, g):
    xf = x.astype(jnp.float32)
    y = xf * lax.rsqrt(jnp.mean(xf * xf, axis=-1, keepdims=True) + NORM_EPS)
    return (y * g.astype(jnp.float32)).astype(x.dtype)


def _wkv7_step(S, inp):
    r, w, k, v, kk, a = inp
    sa = jnp.einsum('bhvk,bhk->bhv', S, -kk)
    S = S * w[:, :, None, :] + sa[..., None] * (kk * a)[:, :, None, :] + v[..., None] * k[:, :, None, :]
    return S, jnp.einsum('bhvk,bhk->bhv', S, r)


def _rwkv7(p, prev_row, s0, prm):
    b, t, _ = p.shape
    f32 = jnp.float32
    prev = jnp.concatenate([prev_row[:, None, :].astype(p.dtype), p[:, :-1]], axis=1)
    s = p + prm['mu_rwkv'] * (prev - p)
    o1 = RW_WIDTH
    o2 = 2 * RW_WIDTH
    o3 = 3 * RW_WIDTH
    o4 = o3 + RW_DECAY_RANK
    o5 = o4 + RW_A_RANK
    r, k, v, wl, al, gl = jnp.split(s, [o1, o2, o3, o4, o5], axis=-1)
    w_log = -jax.nn.softplus(-(prm['w0_decay'] + jnp.tanh(wl) @ prm['w_decay_up']).astype(f32)) - 0.5
    decay = jnp.exp(-jnp.exp(w_log))
    a = jax.nn.sigmoid((prm['a0'] + al @ prm['w_a_up']).astype(f32))
    g = jax.nn.sigmoid(gl) @ prm['w_gate_up']
    hn = (RW_HEADS, RW_HEAD_DIM)
    heads = lambda z: z.astype(f32).reshape(b, t, RW_HEADS, RW_HEAD_DIM)
    r, k, v, decay, a = heads(r), heads(k), heads(v), heads(decay), heads(a)
    kk = k * prm['k_k'].astype(f32).reshape(hn)
    kk = kk / jnp.maximum(jnp.sqrt(jnp.sum(kk * kk, axis=-1, keepdims=True)), 1e-12)
    k = k * (1.0 + (a - 1.0) * prm['k_a'].astype(f32).reshape(hn))
    xs = tuple(jnp.swapaxes(z, 0, 1) for z in (r, decay, k, v, kk, a))
    s_fin, o = lax.scan(_wkv7_step, s0.astype(f32), xs)
    o = jnp.swapaxes(o, 0, 1)
    mean = jnp.mean(o, axis=-1, keepdims=True)
    var = jnp.mean(jnp.square(o - mean), axis=-1, keepdims=True)
    o = (o - mean) * lax.rsqrt(var + RW_GN_EPS)
    o = o.reshape(b, t, RW_WIDTH) * prm['ln_x_w'].astype(f32) + prm['ln_x_b'].astype(f32)
    bonus = jnp.sum(r * k * prm['r_k'].astype(f32), axis=-1, keepdims=True) * v
    o = (o + bonus.reshape(b, t, RW_WIDTH)) * g.astype(f32)
    return o.astype(p.dtype), p[:, -1], s_fin


def _sink_softmax(s, mask, sinks):
    s = jnp.where(mask, s, -jnp.inf)
    sk = sinks.astype(jnp.float32)[:, :, None, None]
    m = jnp.maximum(jnp.max(s, axis=-1, keepdims=True), sk)
    e = jnp.exp(s - m)
    return e / (jnp.sum(e, axis=-1, keepdims=True) + jnp.exp(sk - m))


def _swa_prompt(q, k, v, sinks):
    b, t = q.shape[:2]
    nb = t // WINDOW
    qb = q.reshape(b, nb, WINDOW, SWA_KV_HEADS, SWA_GROUP, SWA_HEAD_DIM)
    pad = jnp.zeros((b, WINDOW, SWA_KV_HEADS, SWA_HEAD_DIM), k.dtype)
    kb = jnp.concatenate([pad, k], axis=1).reshape(b, nb + 1, WINDOW, SWA_KV_HEADS, SWA_HEAD_DIM)
    vb = jnp.concatenate([pad, v], axis=1).reshape(b, nb + 1, WINDOW, SWA_KV_HEADS, SWA_HEAD_DIM)
    kw = jnp.concatenate([kb[:, :-1], kb[:, 1:]], axis=2)
    vw = jnp.concatenate([vb[:, :-1], vb[:, 1:]], axis=2)
    s = jnp.einsum('bnqhgd,bnkhd->bnhgqk', qb, kw, preferred_element_type=jnp.float32) * (SWA_HEAD_DIM ** -0.5)
    i = jnp.arange(WINDOW)[:, None]
    j = jnp.arange(2 * WINDOW)[None, :]
    blk = jnp.arange(nb)[:, None, None]
    mask = (j > i) & (j <= i + WINDOW) & (blk * WINDOW + j >= WINDOW)
    pr = _sink_softmax(s, mask[None, :, None, None], sinks.reshape(SWA_KV_HEADS, SWA_GROUP))
    o = jnp.einsum('bnhgqk,bnkhd->bnqhgd', pr.astype(v.dtype), vw)
    keep = min(WINDOW, t)
    return o.reshape(b, t, SWA_Q_WIDTH), k[:, t - keep:], v[:, t - keep:]


def _swa_step(q, k, v, ck, cv, sinks):
    b, t = q.shape[:2]
    wb = ck.shape[1]
    kf = jnp.concatenate([ck.astype(k.dtype), k], axis=1)
    vf = jnp.concatenate([cv.astype(v.dtype), v], axis=1)
    qg = q.reshape(b, t, SWA_KV_HEADS, SWA_GROUP, SWA_HEAD_DIM)
    s = jnp.einsum('bqhgd,bkhd->bhgqk', qg, kf, preferred_element_type=jnp.float32) * (SWA_HEAD_DIM ** -0.5)
    rel = jnp.arange(t)[:, None] + wb - jnp.arange(wb + t)[None, :]
    mask = (rel >= 0) & (rel < WINDOW)
    pr = _sink_softmax(s, mask, sinks.reshape(SWA_KV_HEADS, SWA_GROUP))
    o = jnp.einsum('bhgqk,bkhd->bqhgd', pr.astype(vf.dtype), vf)
    return o.reshape(b, t, SWA_Q_WIDTH), kf[:, -wb:], vf[:, -wb:]


def _mem_kv(mem, g_mem, w_mem_kv):
    b, m, _ = mem.shape
    kv = _rmsnorm(mem, g_mem) @ w_mem_kv
    mk, mv = jnp.split(kv, 2, axis=-1)
    return mk.reshape(b, m, MEM_HEADS, MEM_HEAD_DIM), mv.reshape(b, m, MEM_HEADS, MEM_HEAD_DIM)


def _mem_attend(q, mk, mv):
    b, t = q.shape[:2]
    qh = q.reshape(b, t, MEM_HEADS, MEM_HEAD_DIM)
    s = jnp.einsum('bthd,bmhd->bhtm', qh, mk.astype(q.dtype), preferred_element_type=jnp.float32) * (MEM_HEAD_DIM ** -0.5)
    pr = jax.nn.softmax(s, axis=-1)
    o = jnp.einsum('bhtm,bmhd->bthd', pr.astype(q.dtype), mv.astype(q.dtype))
    return o.reshape(b, t, MEM_WIDTH)


def _layer(x, swa_fn, mem_k, mem_v, shift_prev, wkv_prev, conv_prev, prm):
    b, t, _ = x.shape
    xn = _rmsnorm(x, prm['g_pre_mix'])
    p = xn @ prm['w_in']
    o1 = RW_PROJ
    o2 = o1 + SWA_Q_WIDTH
    o3 = o2 + SWA_KV_WIDTH
    o4 = o3 + SWA_KV_WIDTH
    o5 = o4 + MEM_WIDTH
    p_rw, q_sw, k_sw, v_sw, q_mem, p_gate = jnp.split(p, [o1, o2, o3, o4, o5], axis=-1)
    o_rw, shift_new, wkv_new = _rwkv7(p_rw, shift_prev, wkv_prev, prm)
    k_sw = k_sw.reshape(b, t, SWA_KV_HEADS, SWA_HEAD_DIM)
    v_sw = v_sw.reshape(b, t, SWA_KV_HEADS, SWA_HEAD_DIM)
    o_sw, kbuf, vbuf = swa_fn(q_sw, k_sw, v_sw)
    o_mem = _mem_attend(q_mem, mem_k, mem_v)
    gates = jax.nn.sigmoid(p_gate.astype(jnp.float32)).astype(x.dtype).reshape(b, t, N_BRANCH, D_MODEL)
    merged = (gates[:, :, 0] * (o_rw @ prm['w_br_rwkv'])
              + gates[:, :, 1] * (o_sw @ prm['w_br_swa'])
              + gates[:, :, 2] * (o_mem @ prm['w_br_mem']))
    h = x + _rmsnorm(merged @ prm['w_o'], prm['g_post_mix'])
    hn = _rmsnorm(h, prm['g_pre_ffn'])
    z_gate, z_val = jnp.split(hn @ prm['w_ffn_up'], 2, axis=-1)
    full = jnp.concatenate([conv_prev.astype(z_gate.dtype), z_gate], axis=1)
    conv = prm['conv_b']
    for j in range(CONV_W):
        conv = conv + prm['conv_w'][j] * full[:, j:j + t]
    f = (jax.nn.gelu(conv) * z_val) @ prm['w_ffn_down']
    y = h + _rmsnorm(f, prm['g_post_ffn'])
    return y, kbuf, vbuf, shift_new, wkv_new, full[:, -(CONV_W - 1):]


def setup_inputs(seed: int = 0) -> dict:
    key = jax.random.key(seed)
    ks = iter(jax.random.split(key, 48))
    f32 = jnp.float32
    nrm = lambda shape, scale: jax.random.normal(next(ks), shape, f32) * scale
    swa_buf = min(WINDOW, PAST_LEN)
    return {
        'x_prompt': nrm((BATCH, SEQ, D_MODEL), 1.0),
        'x_sample': nrm((DEC_BATCH, DEC_SEQ, D_MODEL), 1.0),
        'cache_swa_k': nrm((DEC_BATCH, swa_buf, SWA_KV_HEADS, SWA_HEAD_DIM), 1.0),
        'cache_swa_v': nrm((DEC_BATCH, swa_buf, SWA_KV_HEADS, SWA_HEAD_DIM), 1.0),
        'cache_mem_k': nrm((DEC_BATCH, MEM_TOKENS, MEM_HEADS, MEM_HEAD_DIM), 1.0),
        'cache_mem_v': nrm((DEC_BATCH, MEM_TOKENS, MEM_HEADS, MEM_HEAD_DIM), 1.0),
        'state_rwkv_shift': nrm((DEC_BATCH, RW_PROJ), 1.0),
        'state_rwkv_wkv': nrm((DEC_BATCH, RW_HEADS, RW_HEAD_DIM, RW_HEAD_DIM), 0.3),
        'state_ffn_conv': nrm((DEC_BATCH, CONV_W - 1, D_FF), 1.0),
        'mem_prompt': nrm((BATCH, MEM_TOKENS, D_MODEL), 1.0),
        'g_pre_mix': 1.0 + nrm((D_MODEL,), 0.05),
        'w_in': nrm((D_MODEL, P_IN), D_MODEL ** -0.5),
        'mu_rwkv': jax.random.uniform(next(ks), (RW_PROJ,), f32, 0.1, 0.9),
        'w_decay_up': nrm((RW_DECAY_RANK, RW_WIDTH), 0.5 * RW_DECAY_RANK ** -0.5),
        'w0_decay': nrm((RW_WIDTH,), 0.5),
        'w_a_up': nrm((RW_A_RANK, RW_WIDTH), RW_A_RANK ** -0.5),
        'a0': nrm((RW_WIDTH,), 0.3),
        'w_gate_up': nrm((RW_GATE_RANK, RW_WIDTH), RW_GATE_RANK ** -0.5),
        'k_k': 0.85 + nrm((RW_WIDTH,), 0.05),
        'k_a': 1.0 + nrm((RW_WIDTH,), 0.05),
        'r_k': nrm((RW_HEADS, RW_HEAD_DIM), 0.1),
        'ln_x_w': 1.0 + nrm((RW_WIDTH,), 0.05),
        'ln_x_b': nrm((RW_WIDTH,), 0.02),
        'swa_sinks': nrm((SWA_Q_HEADS,), 0.5),
        'g_mem': 1.0 + nrm((D_MODEL,), 0.05),
        'w_mem_kv': nrm((D_MODEL, 2 * MEM_WIDTH), D_MODEL ** -0.5),
        'w_br_rwkv': nrm((RW_WIDTH, D_MODEL), RW_WIDTH ** -0.5),
        'w_br_swa': nrm((SWA_Q_WIDTH, D_MODEL), SWA_Q_WIDTH ** -0.5),
        'w_br_mem': nrm((MEM_WIDTH, D_MODEL), MEM_WIDTH ** -0.5),
        'w_o': nrm((D_MODEL, D_MODEL), D_MODEL ** -0.5),
        'g_post_mix': 1.0 + nrm((D_MODEL,), 0.05),
        'g_pre_ffn': 1.0 + nrm((D_MODEL,), 0.05),
        'w_ffn_up': nrm((D_MODEL, 2 * D_FF), D_MODEL ** -0.5),
        'conv_w': nrm((CONV_W, D_FF), CONV_W ** -0.5),
        'conv_b': nrm((D_FF,), 0.02),
        'w_ffn_down': nrm((D_FF, D_MODEL), D_FF ** -0.5),
        'g_post_ffn': 1.0 + nrm((D_MODEL,), 0.05),
    }


def reference(x_prompt, x_sample, cache_swa_k, cache_swa_v, cache_mem_k, cache_mem_v,
              state_rwkv_shift, state_rwkv_wkv, state_ffn_conv, mem_prompt,
              g_pre_mix, w_in, mu_rwkv, w_decay_up, w0_decay, w_a_up, a0, w_gate_up,
              k_k, k_a, r_k, ln_x_w, ln_x_b, swa_sinks, g_mem, w_mem_kv,
              w_br_rwkv, w_br_swa, w_br_mem, w_o, g_post_mix, g_pre_ffn,
              w_ffn_up, conv_w, conv_b, w_ffn_down, g_post_ffn):
    prm = dict(g_pre_mix=g_pre_mix, w_in=w_in, mu_rwkv=mu_rwkv, w_decay_up=w_decay_up,
               w0_decay=w0_decay, w_a_up=w_a_up, a0=a0, w_gate_up=w_gate_up, k_k=k_k, k_a=k_a,
               r_k=r_k, ln_x_w=ln_x_w, ln_x_b=ln_x_b, w_br_rwkv=w_br_rwkv, w_br_swa=w_br_swa,
               w_br_mem=w_br_mem, w_o=w_o, g_post_mix=g_post_mix, g_pre_ffn=g_pre_ffn,
               w_ffn_up=w_ffn_up, conv_w=conv_w, conv_b=conv_b, w_ffn_down=w_ffn_down,
               g_post_ffn=g_post_ffn)
    bp = x_prompt.shape[0]
    dt = x_prompt.dtype
    swa_prompt_fn = lambda q, k, v: _swa_prompt(q, k, v, swa_sinks)
    swa_sample_fn = lambda q, k, v: _swa_step(q, k, v, cache_swa_k, cache_swa_v, swa_sinks)
    y_p = x_prompt
    y_s = x_sample
    for _ in range(DEPTH):
        mem_k_p, mem_v_p = _mem_kv(mem_prompt, g_mem, w_mem_kv)
        y_p, swa_k_p, swa_v_p, shift_p, wkv_p, conv_p = _layer(
            y_p, swa_prompt_fn, mem_k_p, mem_v_p,
            jnp.zeros((bp, RW_PROJ), dt),
            jnp.zeros((bp, RW_HEADS, RW_HEAD_DIM, RW_HEAD_DIM), jnp.float32),
            jnp.zeros((bp, CONV_W - 1, D_FF), dt), prm)
        y_s, swa_k_s, swa_v_s, shift_s, wkv_s, conv_s = _layer(
            y_s, swa_sample_fn, cache_mem_k, cache_mem_v,
            state_rwkv_shift, state_rwkv_wkv, state_ffn_conv, prm)
    return (y_p, y_s, swa_k_p, swa_v_p, mem_k_p, mem_v_p, shift_p, wkv_p, conv_p,
            swa_k_s, swa_v_s, shift_s, wkv_s, conv_s)
```

```python
import contextlib
import os
SUB = os.environ.get('SUB', 'abc')
import numpy as np
import concourse.bass as bass
import concourse.mybir as mybir
from concourse.bass_utils import run_bass_kernel_spmd

F32 = mybir.dt.float32
F32R = mybir.dt.float32r
BF16 = mybir.dt.bfloat16
AF = mybir.ActivationFunctionType
ALU = mybir.AluOpType
AX = mybir.AxisListType
ENGS = ("pe", "act", "dve", "pool", "sp")

D = 2048
SEQ = 2048
RWP = 2560
DFF = 5632
NFF = 44
TT = 512
EPS = 1e-6
USE_R = False
USE_SCR = True
PROLOGUE = False


class Buf:
    __slots__ = ("name", "t", "last_w", "reads", "dsem", "dcount", "const", "psum")

    def __init__(self, name, t):
        self.name = name
        self.t = t
        self.last_w = None
        self.reads = []
        self.dsem = None
        self.dcount = 0
        self.const = False
        self.psum = False

    def __getitem__(self, idx):
        return View(self, self.t[idx])


class View:
    __slots__ = ("buf", "ap")

    def __init__(self, buf, ap):
        self.buf = buf
        self.ap = ap

    def bitcast(self, dt):
        return View(self.buf, self.ap.bitcast(dt))

    def rearrange(self, pat, **kw):
        return View(self.buf, self.ap.rearrange(pat, **kw))

    def __getitem__(self, idx):
        return View(self.buf, self.ap[idx])

    @property
    def r(self):
        if not USE_R:
            return self
        return View(self.buf, self.ap.bitcast(F32R))


def _ap(x):
    return x.ap if isinstance(x, View) else x


def _bufs(*xs):
    out = []
    for x in xs:
        if isinstance(x, View) and x.buf not in out:
            out.append(x.buf)
    return out


class Op:
    __slots__ = ("eng", "fn", "waits", "is_dma", "dsem", "dval", "sig", "needs_inc", "ev")


class Prog:
    def __init__(self, nc):
        self.nc = nc
        self.ops = {e: [] for e in ENGS}
        self.all_ops = []
        self.final_events = []

    def _deps(self, reads, writes):
        deps = []
        for b in reads:
            if b.last_w is not None:
                deps.append(b.last_w)
        for b in writes:
            if b.last_w is not None:
                deps.append(b.last_w)
            deps.extend(b.reads)
        return deps

    def _commit(self, o, ev, reads, writes):
        self.ops[o.eng].append(o)
        self.all_ops.append(o)
        for b in writes:
            b.last_w = ev
            b.reads = []
        for b in reads:
            if b not in writes and not b.const:
                b.reads.append(ev)

    def op(self, eng, fn, reads=(), writes=()):
        pr = [b for b in reads if b.psum]
        if pr:
            reads = [b for b in reads if not b.psum]
            writes = list(writes) + [b for b in pr if b not in writes]
        o = Op()
        o.eng, o.fn, o.is_dma = eng, fn, False
        o.waits = self._deps(reads, writes)
        o.needs_inc, o.sig = False, None
        self._commit(o, ("op", o), reads, writes)
        return o

    def dma(self, queue, fn, reads=(), writes=(), final=False, sem_buf=None, extra=()):
        o = Op()
        o.eng, o.fn, o.is_dma = queue, fn, True
        o.waits = self._deps(reads, writes) + list(extra)
        o.needs_inc, o.sig = True, None
        sb = sem_buf if sem_buf is not None else (writes[0] if writes else reads[0])
        if sb.dsem is None:
            sb.dsem = self.nc.alloc_semaphore(name=("d_" + sb.name)[:40])
        sb.dcount += 1
        o.dsem, o.dval = sb.dsem, 16 * sb.dcount
        ev = ("dma", o)
        self._commit(o, ev, reads, writes)
        if final:
            self.final_events.append(ev)
        o.ev = ev
        return o

    def emit(self):
        nc = self.nc
        for o in self.all_ops:
            for kind, d in o.waits:
                if kind == "op" and not (d.eng == o.eng and d.eng == "pe"):
                    d.needs_inc = True
        esem = {}
        for e in ENGS:
            cnt = 0
            for o in self.ops[e]:
                if not o.is_dma and o.needs_inc:
                    cnt += 1
                    o.sig = cnt
            if cnt:
                esem[e] = nc.alloc_semaphore(name="e_" + e)
        finals = self.final_events

        def run(e, eng):
            known = {}

            def wait_all(evs):
                need = {}
                for kind, d in evs:
                    if kind == "op":
                        if d.eng == e and e == "pe":
                            continue
                        s, v = esem[d.eng], d.sig
                    else:
                        s, v = d.dsem, d.dval
                    if known.get(s.num, 0) >= v:
                        continue
                    if s.num not in need or need[s.num][1] < v:
                        need[s.num] = (s, v)
                for k, (s, v) in need.items():
                    eng.wait_ge(s, v)
                    known[k] = v

            for o in self.ops[e]:
                wait_all(o.waits)
                inst = o.fn(eng)
                if o.is_dma:
                    inst.then_inc(o.dsem, 16)
                elif o.needs_inc:
                    inst.then_inc(esem[e], 1)
            if e == "sp":
                wait_all(finals)

        with nc.Block() as block:
            @block.tensor
            def _(eng):
                run("pe", eng)

            @block.scalar
            def _(eng):
                run("act", eng)

            @block.vector
            def _(eng):
                run("dve", eng)

            @block.gpsimd
            def _(eng):
                run("pool", eng)

            @block.sync
            def _(eng):
                run("sp", eng)


def _blk(w, c0, c1):
    K = w.shape[0]
    kc = K // 128
    sub = w[:, c0:c1]
    return np.ascontiguousarray(sub.reshape(kc, 128, c1 - c0).transpose(1, 0, 2).reshape(128, kc * (c1 - c0)))


def _pcol(v, n):
    return np.ascontiguousarray(np.asarray(v, np.float32).reshape(n, 128).T)


class WLayout:
    def __init__(self):
        self.cols = 0
        self.parts = []
        self.idx = {}

    def add(self, key, arr):
        self.idx[key] = (self.cols, arr.shape[1])
        self.parts.append(arr)
        self.cols += arr.shape[1]


def build_wall(inp):
    L = WLayout()
    w_in = inp["w_in"]
    o_q = RWP
    o_k = o_q + 768
    o_v = o_k + 256
    o_m = o_v + 256
    o_g = o_m + 512
    for n in range(20):
        L.add(("rw", n), _blk(w_in, n * 128, (n + 1) * 128))
    for n in range(6):
        L.add(("q", n), _blk(w_in, o_q + n * 128, o_q + (n + 1) * 128))
    for h in range(4):
        kb = w_in[:, o_k + h * 64:o_k + (h + 1) * 64]
        L.add(("kd", h), _blk(np.concatenate([kb, kb], axis=1), 0, 128))
    for n in range(2):
        L.add(("v", n), _blk(w_in, o_v + n * 128, o_v + (n + 1) * 128))
    for n in range(4):
        L.add(("qm", n), _blk(w_in, o_m + n * 128, o_m + (n + 1) * 128))
    br = [inp["w_br_rwkv"], inp["w_br_swa"], inp["w_br_mem"]]
    for n in range(16):
        for i in range(3):
            L.add(("g", i, n), _blk(w_in, o_g + i * D + n * 128, o_g + i * D + (n + 1) * 128))
            L.add(("br", i, n), _blk(br[i], n * 128, (n + 1) * 128))
    for n in range(16):
        L.add(("o", n), _blk(inp["w_o"], n * 128, (n + 1) * 128))
    up = inp["w_ffn_up"]
    dn = inp["w_ffn_down"]
    for kg in range(4):
        for jj in range(11):
            j = kg * 11 + jj
            L.add(("ug", j), _blk(up, j * 128, (j + 1) * 128))
            L.add(("uv", j), _blk(up, DFF + j * 128, DFF + (j + 1) * 128))
        for n in range(16):
            L.add(("dn", kg, n), _blk(dn[kg * 1408:(kg + 1) * 1408], n * 128, (n + 1) * 128))
    for n in range(8):
        L.add(("mkv", n), _blk(inp["w_mem_kv"], n * 128, (n + 1) * 128))
    wall = np.concatenate(L.parts, axis=1)
    L.parts = None
    return L, wall


CI = {}


def build_consts(inp):
    cols = []
    off = [0]

    def add(name, arr):
        arr = np.asarray(arr, np.float32)
        CI[name] = (off[0], arr.shape[1])
        cols.append(arr)
        off[0] += arr.shape[1]

    add("g_pre_mix", _pcol(inp["g_pre_mix"], 16))
    add("g_post_mix", _pcol(inp["g_post_mix"], 16))
    add("g_pre_ffn", _pcol(inp["g_pre_ffn"], 16))
    add("g_post_ffn", _pcol(inp["g_post_ffn"], 16))
    add("g_mem", _pcol(inp["g_mem"], 16))
    add("mu", _pcol(inp["mu_rwkv"], 20))
    for k in ("w0_decay", "a0", "k_k", "k_a", "ln_x_w", "ln_x_b"):
        add(k, _pcol(inp[k], 6))
    add("r_k", _pcol(inp["r_k"].reshape(-1), 6))
    add("sinks", _pcol(np.repeat(inp["swa_sinks"], 64), 6))
    add("conv_w", np.ascontiguousarray(inp["conv_w"].reshape(3, NFF, 128).transpose(2, 0, 1).reshape(128, 3 * NFF)))
    add("conv_b", _pcol(inp["conv_b"], NFF))
    add("wdau", np.concatenate([inp["w_decay_up"], inp["w_a_up"]], axis=0))
    add("wgu", inp["w_gate_up"])
    return np.ascontiguousarray(np.concatenate(cols, axis=1))


def build_program(L, ncols_wall, ncols_consts, dbg=None, stage=99, n_ptiles=4, do_sample=True):
    nc = bass.Bass("TRN2", target_bir_lowering=False)
    dt = nc.dram_tensor
    xp = dt("xp", [SEQ, D], F32, kind="ExternalInput").ap()
    xs = dt("xs", [128, D], F32, kind="ExternalInput").ap()
    memp = dt("memp", [256, D], F32, kind="ExternalInput").ap()
    csk = dt("csk", [16, 128, 256], F32, kind="ExternalInput").ap()
    csv = dt("csv", [16, 128, 256], F32, kind="ExternalInput").ap()
    cmk = dt("cmk", [16, 256, 512], F32, kind="ExternalInput").ap()
    cmv = dt("cmv", [16, 256, 512], F32, kind="ExternalInput").ap()
    sshift = dt("sshift", [16, RWP], F32, kind="ExternalInput").ap()
    swkv = dt("swkv", [16, 6, 128, 64], F32, kind="ExternalInput").ap()
    sconv = dt("sconv", [32, DFF], F32, kind="ExternalInput").ap()
    wall = dt("wall", [128, ncols_wall], F32, kind="ExternalInput").ap()
    consts_d = dt("consts", [128, ncols_consts], F32, kind="ExternalInput").ap()
    wscr = dt("wscr", [128, ncols_wall], BF16, kind="Internal").ap()

    yp = dt("yp", [SEQ // 2, D], F32, kind="ExternalOutput").ap()
    flag_d = dt("flag", [128, 1], F32, kind="ExternalInput").ap()
    ys = dt("ys", [128, D], F32, kind="ExternalOutput").ap()
    o_swk_p = dt("o_swk_p", [128, 256], F32, kind="ExternalOutput").ap()
    o_swv_p = dt("o_swv_p", [128, 256], F32, kind="ExternalOutput").ap()
    o_mk_p = dt("o_mk_p", [256, 512], F32, kind="ExternalOutput").ap()
    o_mv_p = dt("o_mv_p", [256, 512], F32, kind="ExternalOutput").ap()
    o_shift_p = dt("o_shift_p", [20, 128], F32, kind="ExternalOutput").ap()
    o_wkv_p = dt("o_wkv_p", [6, 128, 64], F32, kind="ExternalOutput").ap()
    o_conv_p = dt("o_conv_p", [2, DFF], F32, kind="ExternalOutput").ap()
    o_swk_s = dt("o_swk_s", [16, 128, 256], F32, kind="ExternalOutput").ap()
    o_swv_s = dt("o_swv_s", [16, 128, 256], F32, kind="ExternalOutput").ap()
    o_shift_s = dt("o_shift_s", [16, RWP], F32, kind="ExternalOutput").ap()
    o_wkv_s = dt("o_wkv_s", [16, 6, 128, 64], F32, kind="ExternalOutput").ap()
    o_conv_s = dt("o_conv_s", [32, DFF], F32, kind="ExternalOutput").ap()
    dbg_aps = {}
    if dbg:
        for k, shp in dbg.items():
            dbg_aps[k] = dt("dbg_" + k, list(shp), F32, kind="ExternalOutput").ap()

    es = contextlib.ExitStack()
    P = Prog(nc)

    def sb(name, shape, dtp=F32):
        return Buf(name, es.enter_context(nc.sbuf_tensor(name, list(shape), dtp)))

    def psb(name, shape):
        b_ = Buf(name, es.enter_context(nc.psum_tensor(name, list(shape), F32)))
        b_.psum = True
        return b_

    def mm(out, lhsT, rhs, start=True, stop=True):
        P.op("pe", lambda e: e.matmul(_ap(out), lhsT=_ap(lhsT), rhs=_ap(rhs), start=start, stop=stop,
                                      skip_group_check=True),
             _bufs(lhsT, rhs), _bufs(out))

    def tr(out, in_):
        P.op("pe", lambda e: e.transpose(_ap(out), _ap(in_), _ap(ident[0:in_.ap.shape[0], 0:in_.ap.shape[0]])),
             _bufs(in_, ident[:]), _bufs(out))

    def act(out, in_, func, bias=None, scale=None, eng="act"):
        kw = {}
        if bias is not None:
            kw["bias"] = _ap(bias)
        if scale is not None:
            kw["scale"] = _ap(scale)
        P.op(eng, lambda e: e.activation(out=_ap(out), in_=_ap(in_), func=func, **kw),
             _bufs(in_, bias, scale), _bufs(out))

    def tt(out, in0, in1, op, eng="dve"):
        P.op(eng, lambda e: e.tensor_tensor(out=_ap(out), in0=_ap(in0), in1=_ap(in1), op=op),
             _bufs(in0, in1), _bufs(out))

    def ts(out, in0, s1, s2, op0, op1=None, eng="dve"):
        if op1 is None:
            P.op(eng, lambda e: e.tensor_scalar(out=_ap(out), in0=_ap(in0), scalar1=_ap(s1), scalar2=None, op0=op0),
                 _bufs(in0, s1), _bufs(out))
        else:
            P.op(eng, lambda e: e.tensor_scalar(out=_ap(out), in0=_ap(in0), scalar1=_ap(s1), scalar2=_ap(s2),
                                                op0=op0, op1=op1),
                 _bufs(in0, s1, s2), _bufs(out))

    def stt(out, in0, scalar, in1, op0, op1):
        P.op("dve", lambda e: e.scalar_tensor_tensor(out=_ap(out), in0=_ap(in0), scalar=_ap(scalar), in1=_ap(in1),
                                                     op0=op0, op1=op1),
             _bufs(in0, scalar, in1), _bufs(out))

    def cp(out, in_, eng="dve"):
        if eng == "act":
            P.op("act", lambda e: e.copy(out=_ap(out), in_=_ap(in_)), _bufs(in_), _bufs(out))
        else:
            P.op(eng, lambda e: e.tensor_copy(out=_ap(out), in_=_ap(in_)), _bufs(in_), _bufs(out))

    def recip(out, in_):
        P.op("dve", lambda e: e.reciprocal(out=_ap(out), in_=_ap(in_)), _bufs(in_), _bufs(out))

    def memset(out, val, eng="pool"):
        P.op(eng, lambda e: e.memset(_ap(out), val), (), _bufs(out))

    def dma_in(out, in_ap, q="sp"):
        P.dma(q, lambda e: e.dma_start(out=_ap(out), in_=in_ap), (), _bufs(out))

    def dma_out(out_ap, in_, q="act"):
        P.dma(q, lambda e: e.dma_start(out=out_ap, in_=_ap(in_)), _bufs(in_), (), final=True)

    ident = sb("ident", [128, 128])
    ones = sb("ones", [128, 128])
    ones_r = sb("ones_r", [128, 128], BF16)
    blk1 = sb("blk1", [128, 128])
    cst = sb("cst", [128, ncols_consts])
    epsb = sb("epsb", [128, 1])
    omka = sb("omka", [128, 6])
    esink = sb("esink", [128, 6])
    for b_ in (ident, ones, ones_r, blk1, cst, epsb, omka, esink):
        b_.const = True

    def C(name, c0=None, c1=None):
        o, n = CI[name]
        if c0 is None:
            return cst[:, o:o + n]
        return cst[:, o + c0:o + (c1 if c1 is not None else c0 + 1)]

    dma_in(cst[:], consts_d[:, :])
    flagt = sb("flagt", [128, 1])
    flagt.const = True
    dma_in(flagt[:], flag_d[:, :])
    memset(ident[:], 0.0)
    P.op("pool", lambda e: e.affine_select(out=ident.t[:], in_=ident.t[:], pattern=[[-1, 128]],
                                           compare_op=ALU.not_equal, fill=1.0, base=0, channel_multiplier=1),
         [ident], [ident])
    memset(ones[:], 1.0)
    memset(ones_r[:], 1.0)
    memset(epsb[:], EPS)
    memset(blk1[:], 0.0)
    memset(blk1[0:64, 0:64], 1.0)
    memset(blk1[64:128, 64:128], 1.0)
    ts(omka[:], C("k_a"), -1.0, 1.0, ALU.mult, ALU.add)
    act(esink[:], C("sinks"), AF.Exp)

    def tri_mask(name, n, m, pattern_step, chan_mult, base, cmp):
        t = sb(name, [n, m])
        memset(t[:], 1.0)
        P.op("pool", lambda e: e.affine_select(out=t.t[:], in_=t.t[:], pattern=[[pattern_step, m]],
                                               compare_op=cmp, fill=0.0, base=base, channel_multiplier=chan_mult),
             [t], [t])
        t.const = True
        return t

    maskD = tri_mask("maskD", 128, 128, 1, -1, 0, ALU.is_ge)
    maskP = tri_mask("maskP", 128, 128, -1, 1, -1, ALU.is_ge)

    NPOOL = 42
    pool_free = [sb(f"cb{i}", [128, TT]) for i in range(NPOOL)]
    NRPOOL = 50
    rpool_free = [sb(f"rb{i}", [128, TT], BF16) for i in range(NRPOOL)]

    def ralloc(n=1):
        assert len(rpool_free) >= n, "bf16 pool exhausted"
        return [rpool_free.pop() for _ in range(n)]

    def rfree(bs):
        for b_ in bs:
            rpool_free.insert(0, b_)

    def alloc(n=1):
        assert len(pool_free) >= n, "chunk pool exhausted"
        r = [pool_free.pop() for _ in range(n)]
        return r

    def free(bs):
        for b_ in bs:
            pool_free.insert(0, b_)

    NSLOT = 4
    wslots = [sb(f"ws{i}", [128, 2048], BF16) for i in range(NSLOT)]
    wctr = [0]

    scr_ev = {}
    whs = [Buf(f"wh{i}", None) for i in range(NSLOT)]

    def wload(key):
        off, n = L.idx[key]
        s = wslots[wctr[0] % NSLOT]
        wh = whs[wctr[0] % NSLOT]
        wctr[0] += 1
        if key in scr_ev:
            P.dma("sp", lambda e: e.dma_start(out=s.t[:, 0:n], in_=wscr[:, off:off + n]), (), [s], extra=[scr_ev[key]],
                  sem_buf=wh)
        else:
            P.dma("pool", lambda e: e.dma_start(out=s.t[:, 0:n], in_=wall[:, off:off + n]), (), [s])
            if key[0] != "mkv" and USE_SCR:
                o = P.dma("sp", lambda e: e.dma_start(out=wscr[:, off:off + n], in_=s.t[:, 0:n]), [s], (), sem_buf=wh)
                scr_ev[key] = o.ev
        return s, n // 128

    banks = [psb(f"bank{i}", [128, 512]) for i in range(8)]
    gctr = [0]

    def gbank():
        b_ = banks[(0, 1, 6, 7)[gctr[0] % 4]]
        gctr[0] += 1
        return b_

    def gemm_fm(key, act_chunks, n_tok, evac, col0=0):
        s, kc = wload(key)
        assert kc == len(act_chunks), (key, kc, len(act_chunks))
        pb_ = gbank()
        for k in range(kc):
            mm(pb_[:, 0:n_tok], s[:, k * 128:(k + 1) * 128], act_chunks[k][:, col0:col0 + n_tok].r, start=(k == 0),
               stop=(k == kc - 1))
        evac(pb_[:, 0:n_tok])

    def dbg_out(name, view):
        if name in dbg_aps:
            dma_out(dbg_aps[name], view)

    def rms_rstd(chunks, n_tok):
        ssb = banks[3]
        nck = len(chunks)
        sq = ralloc(4)
        for c in range(nck):
            s_ = sq[c % 4]
            act(s_[:, 0:n_tok].r, chunks[c][:, 0:n_tok], AF.Square)
            mm(ssb[:, 0:n_tok], ones_r[:], s_[:, 0:n_tok].r, start=(c == 0), stop=(c == nck - 1))
        rfree(sq)
        rs = alloc(1)[0]
        act(rs[:, 0:n_tok], ssb[:, 0:n_tok], AF.Sqrt, bias=epsb[:], scale=1.0 / (nck * 128))
        recip(rs[:, 0:n_tok], rs[:, 0:n_tok])
        return rs

    gneps = sb("gneps", [128, 1])
    memset(gneps[:], 6.4e-4)
    gneps.const = True
    shiftI = sb("shiftI", [128, 128])
    memset(shiftI[:], 0.0)
    P.op("pool", lambda e: e.affine_select(out=shiftI.t[:], in_=shiftI.t[:], pattern=[[-1, 128]],
                                           compare_op=ALU.not_equal, fill=1.0, base=64, channel_multiplier=1),
         [shiftI], [shiftI])
    shiftI.const = True

    def build_mask5(name, CLv):
        t = sb(name, [CLv, 10 * CLv])
        memset(t[:], 1.0)
        for h2 in range(2):
            for b5 in range(5):
                o0 = (h2 * 5 + b5) * CLv
                if b5 in (0, 1):
                    base, cm, st = -1, -1, 1
                elif b5 in (3, 4):
                    base, cm, st = 0, -1, 1
                else:
                    base, cm, st = -1, 1, -1
                P.op("pool", lambda e, o0=o0, base=base, cm=cm, st=st: e.affine_select(
                    out=t.t[:, o0:o0 + CLv], in_=t.t[:, o0:o0 + CLv], pattern=[[st, CLv]],
                    compare_op=ALU.is_ge, fill=0.0, base=base, channel_multiplier=cm), [t], [t])
        t.const = True
        return t

    mask5p = build_mask5("mask5p", 64)
    mask5s = build_mask5("mask5s", 8)

    def build_rmask(name, CLv):
        t = sb(name, [128, TT])
        memset(t[:], 1.0)
        memset(t[:, :].rearrange("p (c t) -> p c t", t=CLv)[:, :, 0:1], 0.0)
        t.const = True
        return t

    rmaskp = build_rmask("rmaskp", 64)
    rmasks = build_rmask("rmasks", 8)


    Gb = [[sb(f"Gb{p_}{h_}", [64, 320]) for h_ in range(2)] for p_ in range(3)]
    TOKb = [sb(f"TOK{p_}", [64, 384]) for p_ in range(3)]
    TTb = [sb(f"TTb{p_}", [64, 128]) for p_ in range(3)]
    XYb = [[sb(f"XYb{q_}{l}", [64, 256], BF16) for l in range(2)] for q_ in range(2)]
    TThb = [sb(f"TTh{p_}", [64, 128], BF16) for p_ in range(3)]
    Wbb = [sb("Wb0", [64, 128]), sb("Wb1", [64, 128])]
    zpb = sb("zpb", [128, 8])
    cstg = sb("cstg", [128, 128])
    cout = sb("cout", [32, 256])
    stmp = sb("stmp", [128, 256])
    memset(stmp[:], 0.0)
    dummy = sb("dmy_a", [1, 4])
    dummy2 = sb("dmy_b", [1, 4])

    def rep4(name, src):
        t = sb(name, [128, 384])
        for k in range(3):
            cp(t[:, k * 128:(k + 1) * 128], src[:, :], eng="pool")
        t.const = True
        return t
    maskD4 = rep4("maskD4", maskD)
    maskP4 = rep4("maskP4", maskP)
    maskC96 = sb("maskC96", [128, 96])
    memset(maskC96[:], 1.0)
    P.op("pool", lambda e: e.affine_select(out=maskC96.t[:], in_=maskC96.t[:], pattern=[[0, 12], [-1, 8]],
                                           compare_op=ALU.is_ge, fill=0.0, base=-1, channel_multiplier=1),
         [maskC96], [maskC96])
    maskC96.const = True
    maskN = sb("maskN", [128, 128])
    memset(maskN[:], 1.0)
    for (pat, base, cm) in (([[-8, 16], [0, 8]], 0, 1), ([[8, 16], [0, 8]], 7, -1), ([[8, 16], [1, 8]], 0, -1)):
        P.op("pool", lambda e, pat=pat, base=base, cm=cm: e.affine_select(
            out=maskN.t[:], in_=maskN.t[:], pattern=pat, compare_op=ALU.is_ge, fill=0.0, base=base,
            channel_multiplier=cm), [maskN], [maskN])
    maskN4 = rep4("maskN4", maskN)
    Hz = sb("Hz", [128, 6, 2, 64])
    shc = sb("shc", [128, 20])
    czc = sb("czc", [128, NFF, 2])
    kdprev = sb("kdprev", [128, 4, 128])
    vprev = sb("vprev", [128, 256])
    memKT = sb("memKT", [128, 4, 256])
    memV = sb("memV", [128, 2, 512])
    memset(Hz[:], 0.0)
    memset(shc[:], 0.0)
    memset(czc[:], 0.0)
    memset(kdprev[:], 0.0)
    memset(vprev[:], 0.0)

    def mem_kv_stage():
        mt = alloc(8)
        for blk in range(2):
            for q in range(4):
                dma_in(mt[blk * 4 + q][:], memp[blk * 128:(blk + 1) * 128, q * 512:(q + 1) * 512])
        mT = alloc(16)
        for c in range(16):
            pb_ = banks[2]
            for blk in range(2):
                tr(pb_[:, blk * 128:(blk + 1) * 128], mt[blk * 4 + c // 4][:, (c % 4) * 128:(c % 4 + 1) * 128])
            cp(mT[c][:, 0:256], pb_[:, 0:256], eng="act")
        free(mt)
        rs = rms_rstd(mT, 256)
        mTn = ralloc(16)
        for c in range(16):
            stt(mTn[c][:, 0:256], mT[c][:, 0:256], C("g_mem", c), rs[:, 0:256], ALU.mult, ALU.mult)
        free([rs])
        free(mT)
        kvT = alloc(8)
        for n in range(8):
            def ev(ps, n=n):
                cp(kvT[n][:, 0:256], ps, eng="act")
            gemm_fm(("mkv", n), mTn, 256, ev)
        rfree(mTn)
        for h in range(4):
            cp(memKT[:, h, :].r, kvT[h][:, 0:256])
        kvtok = alloc(4)
        for blk in range(2):
            for half in range(2):
                pb_ = banks[2]
                for q in range(4):
                    n = half * 4 + q
                    tr(pb_[:, q * 128:(q + 1) * 128], kvT[n][:, blk * 128:(blk + 1) * 128])
                cp(kvtok[blk * 2 + half][:], pb_[:], eng="act")
            cp(memV[:, blk, :].r, kvtok[blk * 2 + 1][:])
            dma_out(o_mk_p[blk * 128:(blk + 1) * 128, :], kvtok[blk * 2][:])
            dma_out(o_mv_p[blk * 128:(blk + 1) * 128, :], kvtok[blk * 2 + 1][:])
        free(kvT)
        free(kvtok)

    def rwkv_stage(kind, ti, pT, orwb, NT, NSEQ, SL, CL, NCH, prompt, last_p, o_from=0):
        if not prompt:
            sst = alloc(5)
            for q_ in range(5):
                dma_in(sst[q_][0:16, :], sshift[:, q_ * 512:(q_ + 1) * 512])
            shT = alloc(1)[0]
            for c in range(20):
                pb_ = banks[2]
                tr(pb_[:, 0:16], sst[c // 4][0:16, (c % 4) * 128:(c % 4 + 1) * 128])
                cp(shT[:, c * 16:(c + 1) * 16], pb_[:, 0:16])
            free(sst)
            shout = alloc(5)
        dtmp = alloc(1)[0]
        for c in range(20):
            pv = pT[c][:, 0:NT].rearrange("p (b t) -> p b t", b=NSEQ)
            dv = dtmp[:, 0:NT].rearrange("p (b t) -> p b t", b=NSEQ)
            tt(dv[:, :, 1:SL], pv[:, :, 0:SL - 1], pv[:, :, 1:SL], ALU.subtract)
            if prompt:
                tt(dtmp[:, 0:1], shc[:, c:c + 1], pT[c][:, 0:1], ALU.subtract)
                cp(shc[:, c:c + 1], pT[c][:, NT - 1:NT], eng="dve")
            else:
                tt(dv[:, :, 0:1], shT[:, c * 16:(c + 1) * 16].rearrange("p (b t) -> p b t", t=1), pv[:, :, 0:1],
                   ALU.subtract)
                pb_ = banks[2]
                lastv = alloc(1)[0]
                cp(lastv[:, 0:16].rearrange("p (b t) -> p b t", t=1), pv[:, :, SL - 1:SL], eng="dve")
                tr(pb_[0:16, 0:128], lastv[:, 0:16])
                cp(shout[c // 4][0:16, (c % 4) * 128:(c % 4 + 1) * 128], pb_[0:16, 0:128], eng="act")
                free([lastv])
            stt(pT[c][:, 0:NT], dtmp[:, 0:NT], C("mu", c), pT[c][:, 0:NT], ALU.mult, ALU.add)
        free([dtmp])
        if not prompt:
            free([shT])
            for q_ in range(5):
                dma_out(o_shift_s[:, q_ * 512:(q_ + 1) * 512], shout[q_][0:16, :])
            free(shout)
        if last_p:
            pb_ = banks[2]
            tr(pb_[0:20, 0:128], shc[:, 0:20])
            so = alloc(1)[0]
            cp(so[0:20, 0:128], pb_[0:20, 0:128])
            dma_out(o_shift_p[:, :], so[0:20, 0:128])
            free([so])

        RWS = int(os.environ.get('RWS', 9)) if not prompt else 9
        if RWS < 2:
            return
        lr = pT[18]
        tanh_wl = alloc(1)[0]
        act(tanh_wl[0:64, 0:NT], lr[0:64, 0:NT], AF.Tanh)
        sig_gl = alloc(1)[0]
        act(sig_gl[:, 0:NT], pT[19][:, 0:NT], AF.Sigmoid)
        wdau = C("wdau")
        wgu = C("wgu")
        m5 = mask5p if prompt else mask5s
        rmask = rmaskp if prompt else rmasks
        NLEV = 6 if prompt else 3

        for j in range(6):
            r_, k_, v_ = pT[j], pT[6 + j], pT[12 + j]
            js = slice(j * 128, (j + 1) * 128)
            pb_ = banks[3]
            mm(pb_[:, 0:NT], wdau[0:64, js], tanh_wl[0:64, 0:NT])
            logw = alloc(1)[0]
            act(logw[:, 0:NT], pb_[:, 0:NT], AF.Sigmoid, bias=C("w0_decay", j))
            ts(logw[:, 0:NT], logw[:, 0:NT], -0.6065306597126334, None, ALU.mult, eng="dve")
            pb_ = banks[3]
            mm(pb_[:, 0:NT], wdau[64:128, js], lr[64:128, 0:NT])
            a_ = alloc(1)[0]
            act(a_[:, 0:NT], pb_[:, 0:NT], AF.Sigmoid, bias=C("a0", j))
            kk = alloc(1)[0]
            ts(kk[:, 0:NT], k_[:, 0:NT], C("k_k", j), None, ALU.mult)
            sq_ = alloc(1)[0]
            tt(sq_[:, 0:NT], kk[:, 0:NT], kk[:, 0:NT], ALU.mult, eng="dve")
            pb_ = banks[3]
            mm(pb_[:, 0:NT], blk1[:], sq_[:, 0:NT])
            act(sq_[:, 0:NT], pb_[:, 0:NT], AF.Sqrt)
            ts(sq_[:, 0:NT], sq_[:, 0:NT], 1e-12, None, ALU.max)
            recip(sq_[:, 0:NT], sq_[:, 0:NT])
            tt(kk[:, 0:NT], kk[:, 0:NT], sq_[:, 0:NT], ALU.mult)
            ts(sq_[:, 0:NT], a_[:, 0:NT], C("k_a", j), omka[:, j:j + 1], ALU.mult, ALU.add)
            tt(k_[:, 0:NT], k_[:, 0:NT], sq_[:, 0:NT], ALU.mult)
            stt(sq_[:, 0:NT], r_[:, 0:NT], C("r_k", j), k_[:, 0:NT], ALU.mult, ALU.mult)
            pb_ = banks[3]
            mm(pb_[:, 0:NT], blk1[:], sq_[:, 0:NT])
            bonus = alloc(1)[0]
            tt(bonus[:, 0:NT], pb_[:, 0:NT], v_[:, 0:NT], ALU.mult)
            cum = alloc(1)[0]
            P.op("dve", lambda e, cum=cum, logw=logw: e.tensor_tensor_scan(
                out=cum.t[:, 0:NT], data0=rmask.t[:, 0:NT], data1=logw.t[:, 0:NT], initial=0.0,
                op0=ALU.mult, op1=ALU.add), [rmask, logw], [cum])
            At, Rt, Bt, Kt, e1 = alloc(5)
            act(e1[:, 0:NT], cum[:, 0:NT], AF.Exp)
            tt(Rt[:, 0:NT], r_[:, 0:NT], e1[:, 0:NT], ALU.mult)
            tt(sq_[:, 0:NT], cum[:, 0:NT], logw[:, 0:NT], ALU.subtract, eng="dve")
            act(sq_[:, 0:NT], sq_[:, 0:NT], AF.Exp)
            stt(At[:, 0:NT], kk[:, 0:NT], -1.0, sq_[:, 0:NT], ALU.mult, ALU.mult)
            act(sq_[:, 0:NT], cum[:, 0:NT], AF.Exp, scale=-1.0)
            tt(Kt[:, 0:NT], k_[:, 0:NT], sq_[:, 0:NT], ALU.mult)
            tt(kk[:, 0:NT], kk[:, 0:NT], a_[:, 0:NT], ALU.mult, eng="dve")
            tt(Bt[:, 0:NT], kk[:, 0:NT], sq_[:, 0:NT], ALU.mult)
            free([kk, a_, logw, cum])
            Atb, Rtb, Btb, Ktb = ralloc(4)
            cp(Atb[:, 0:NT], At[:, 0:NT], eng="act")
            cp(Btb[:, 0:NT], Bt[:, 0:NT])
            cp(Ktb[:, 0:NT], Kt[:, 0:NT], eng="act")
            if o_from < NCH:
                cp(Rtb[:, 0:NT], Rt[:, 0:NT])

            if not prompt:
                s_in = alloc(2)
                for half in range(2):
                    dma_in(s_in[half][:, :].rearrange("p (b k) -> p b k", b=8),
                           swkv[half * 8:(half + 1) * 8, j].rearrange("b p k -> p b k"))
                s_out = alloc(2)

            ops_ = banks[4]

            def P1(c, par):
                cs = slice(c * CL, (c + 1) * CL)
                gpsb = [banks[5], banks[0]]
                for h2 in range(2):
                    hp = slice(h2 * 64, (h2 + 1) * 64)
                    gps = gpsb[h2]
                    ng = 5 if c >= o_from else 3
                    mm(gps[0:CL, 0 * CL:1 * CL], Btb[hp, cs], Atb[hp, cs])
                    mm(gps[0:CL, 1 * CL:2 * CL], Ktb[hp, cs], Atb[hp, cs])
                    mm(gps[0:CL, 2 * CL:3 * CL], Atb[hp, cs], Btb[hp, cs])
                    if c >= o_from:
                        mm(gps[0:CL, 3 * CL:4 * CL], Btb[hp, cs], Rtb[hp, cs])
                        mm(gps[0:CL, 4 * CL:5 * CL], Ktb[hp, cs], Rtb[hp, cs])
                    tt(Gb[par][h2][0:CL, 0:ng * CL], gps[0:CL, 0:ng * CL], m5[0:CL, 0:ng * CL], ALU.mult)
                tps = banks[6]
                tr(tps[0:CL, 0:128], v_[:, cs])
                tr(tps[0:CL, 128:256], Bt[:, cs])
                tr(tps[0:CL, 256:384], Kt[:, cs])
                cp(TOKb[par][0:CL, 0:384], tps[0:CL, 0:384], eng="act")
                yield
                GG = [Gb[par][h2][0:CL, 0:5 * CL] for h2 in range(2)]
                TTp = TTb[par]
                TTh = TThb[par]
                for h2 in range(2):
                    tt(TTp[0:CL, h2 * CL:(h2 + 1) * CL], GG[h2][:, 0:CL], ident[0:CL, 0:CL], ALU.add)
                cp(TTh[0:CL, 0:2 * CL], TTp[0:CL, 0:2 * CL], eng="act")
                XYs = XYb[c % 2]
                sqbank = banks[7] if c % 2 == 0 else banks[3]
                xy = XYs[0]
                for h2 in range(2):
                    cp(xy[0:CL, (h2 * 2) * CL:(h2 * 2 + 1) * CL], GG[h2][:, 0:CL], eng="act")
                    cp(xy[0:CL, (h2 * 2 + 1) * CL:(h2 * 2 + 2) * CL], GG[h2][:, 2 * CL:3 * CL])
                X = [xy[0:CL, (h2 * 2) * CL:(h2 * 2 + 1) * CL] for h2 in range(2)]
                Y = [xy[0:CL, (h2 * 2 + 1) * CL:(h2 * 2 + 2) * CL] for h2 in range(2)]
                for s_ in range(1, NLEV + 1):
                    sps = sqbank
                    do_sq = s_ <= NLEV - 1
                    do_tu = s_ >= 2
                    if do_sq:
                        for h2 in range(2):
                            mm(sps[0:CL, (h2 * 2) * CL:(h2 * 2 + 1) * CL], Y[h2], X[h2])
                            mm(sps[0:CL, (h2 * 2 + 1) * CL:(h2 * 2 + 2) * CL], X[h2], Y[h2])
                    if do_tu:
                        for h2 in range(2):
                            mm(sps[0:CL, (4 + h2) * CL:(5 + h2) * CL], Y[h2], TTh[0:CL, h2 * CL:(h2 + 1) * CL])
                    if do_sq:
                        xy = XYs[s_ % 2]
                        cp(xy[0:CL, 0:4 * CL], sps[0:CL, 0:4 * CL], eng="act")
                    if do_tu:
                        tt(TTp[0:CL, 0:2 * CL], TTp[0:CL, 0:2 * CL], sps[0:CL, 4 * CL:6 * CL], ALU.add)
                        if s_ < NLEV:
                            cp(TTh[0:CL, 0:2 * CL], TTp[0:CL, 0:2 * CL])
                    if do_sq:
                        X = [xy[0:CL, (h2 * 2) * CL:(h2 * 2 + 1) * CL] for h2 in range(2)]
                        Y = [xy[0:CL, (h2 * 2 + 1) * CL:(h2 * 2 + 2) * CL] for h2 in range(2)]
                    yield

            def P2(c, par):
                cs = slice(c * CL, (c + 1) * CL)
                GG = [Gb[par][h2][0:CL, 0:5 * CL] for h2 in range(2)]
                TOK = TOKb[par]
                TTp = TTb[par]
                Vt = [TOK[0:CL, h2 * 64:(h2 + 1) * 64] for h2 in range(2)]
                Btk = [TOK[0:CL, 128 + h2 * 64:128 + (h2 + 1) * 64] for h2 in range(2)]
                Ktk = [TOK[0:CL, 256 + h2 * 64:256 + (h2 + 1) * 64] for h2 in range(2)]
                if not prompt:
                    pb_ = banks[2]
                    sv = s_in[c // 8][:, (c % 8) * 64:(c % 8 + 1) * 64]
                    tr(pb_[0:64, 0:128], sv)
                    cp(Hz[0:64, j, 0, :], pb_[0:64, 0:64])
                    cp(stmp[0:64, 0:64], pb_[0:64, 64:128], eng="act")
                    mm(pb_[:, 128:192], shiftI[:, :], stmp[:, 0:64])
                    cp(Hz[64:128, j, 1, :], pb_[64:128, 128:192])
                wb = banks[1]
                for h2 in range(2):
                    mm(wb[0:CL, h2 * 64:(h2 + 1) * 64], At[:, cs], Hz[:, j, h2, :], start=True, stop=False)
                    mm(wb[0:CL, h2 * 64:(h2 + 1) * 64], GG[h2][:, 1 * CL:2 * CL], Vt[h2], start=False, stop=True)
                cp(Wbb[0][0:CL, 0:128], wb[0:CL, 0:128])
                yield
                for h2 in range(2):
                    mm(wb[0:CL, 128 + h2 * 64:128 + (h2 + 1) * 64], TTp[0:CL, h2 * CL:(h2 + 1) * CL],
                       Wbb[0][0:CL, h2 * 64:(h2 + 1) * 64])
                cp(Wbb[1][0:CL, 0:128], wb[0:CL, 128:256], eng="act")
                U = [Wbb[1][0:CL, h2 * 64:(h2 + 1) * 64] for h2 in range(2)]
                yield
                for h2 in (range(2) if c >= o_from else []):
                    hp = slice(h2 * 64, (h2 + 1) * 64)
                    mm(ops_[hp, cs], Hz[:, j, h2, :], Rt[:, cs], start=True, stop=False)
                    mm(ops_[hp, cs], U[h2], GG[h2][:, 3 * CL:4 * CL], start=False, stop=False)
                    mm(ops_[hp, cs], Vt[h2], GG[h2][:, 4 * CL:5 * CL], start=False, stop=True)
                for h2 in range(2):
                    hp = slice(h2 * 64, (h2 + 1) * 64)
                    mm(wb[hp, 256:320], ident[:, hp], Hz[:, j, h2, :], start=True, stop=False)
                    mm(wb[hp, 256:320], Btk[h2], U[h2], start=False, stop=False)
                    mm(wb[hp, 256:320], Ktk[h2], Vt[h2], start=False, stop=True)
                gcol = slice((c + 1) * CL - 1, (c + 1) * CL)
                act(Hz[0:64, j, 0, :], wb[0:64, 256:320], AF.Copy, scale=e1[0:64, gcol])
                act(Hz[64:128, j, 1, :], wb[64:128, 256:320], AF.Copy, scale=e1[64:128, gcol])
                if not prompt:
                    pb_ = banks[2]
                    mm(pb_[0:64, 256:320], Hz[:, j, 0, :], ident[:, 0:64])
                    mm(pb_[0:64, 320:384], Hz[:, j, 1, :], ident[:, 64:128])
                    so_ = s_out[c // 8]
                    cc = slice((c % 8) * 64, (c % 8 + 1) * 64)
                    cp(so_[0:64, cc], pb_[0:64, 256:320])
                    cp(stmp[0:64, 128:192], pb_[0:64, 320:384], eng="act")
                    mm(pb_[:, 192:256], shiftI[:, :], stmp[:, 128:192])
                    cp(so_[64:128, cc], pb_[64:128, 192:256])
                yield

            def drive(gens):
                gens = list(gens)
                while gens:
                    for g_ in list(gens):
                        try:
                            next(g_)
                        except StopIteration:
                            gens.remove(g_)

            active = []
            p1_done = set()
            nxt1, nxt2, p2_done = 0, 0, -1
            while nxt2 < NCH or active:
                while (nxt1 < NCH and sum(1 for a in active if a[0] == 1) < 2 and nxt1 <= p2_done + 3):
                    active.append((1, nxt1, P1(nxt1, nxt1 % 3)))
                    nxt1 += 1
                if nxt2 < NCH and not any(a[0] == 2 for a in active) and nxt2 in p1_done:
                    active.append((2, nxt2, P2(nxt2, nxt2 % 3)))
                    nxt2 += 1
                for a in list(active):
                    try:
                        next(a[2])
                    except StopIteration:
                        active.remove(a)
                        if a[0] == 1:
                            p1_done.add(a[1])
                        else:
                            p2_done = a[1]
            if not prompt:
                for half in range(2):
                    dma_out(o_wkv_s[half * 8:(half + 1) * 8, j].rearrange("b p k -> p b k"),
                            s_out[half][:, :].rearrange("p (b k) -> p b k", b=8))
                free(s_in + s_out)
            free([e1, At, Rt, Bt, Kt])
            rfree([Atb, Rtb, Btb, Ktb])
            PC0 = o_from * CL
            if o_from < NCH:
                orw = pT[j]
                cp(orw[:, PC0:NT], ops_[:, PC0:NT], eng="act")
                pb_ = banks[3]
                mm(pb_[:, PC0:NT], blk1[:], orw[:, PC0:NT])
                stt(orw[:, PC0:NT], pb_[:, PC0:NT], -1.0 / 64, orw[:, PC0:NT], ALU.mult, ALU.add)
                tt(sq_[:, PC0:NT], orw[:, PC0:NT], orw[:, PC0:NT], ALU.mult, eng="dve")
                pb_ = banks[3]
                mm(pb_[:, PC0:NT], blk1[:], sq_[:, PC0:NT])
                act(sq_[:, PC0:NT], pb_[:, PC0:NT], AF.Sqrt, bias=gneps[:], scale=1.0 / 64)
                recip(sq_[:, PC0:NT], sq_[:, PC0:NT])
                tt(orw[:, PC0:NT], orw[:, PC0:NT], sq_[:, PC0:NT], ALU.mult)
                ts(orw[:, PC0:NT], orw[:, PC0:NT], C("ln_x_w", j), C("ln_x_b", j), ALU.mult, ALU.add)
                tt(orw[:, PC0:NT], orw[:, PC0:NT], bonus[:, PC0:NT], ALU.add)
                pb_ = banks[3]
                mm(pb_[:, PC0:NT], wgu[:, js], sig_gl[:, PC0:NT])
                tt(orwb[j][:, PC0:NT], orw[:, PC0:NT], pb_[:, PC0:NT], ALU.mult)
            free([sq_, bonus])
        free([tanh_wl, sig_gl])
        if last_p:
            so = alloc(1)[0]
            for j in range(6):
                pb_ = banks[2]
                mm(pb_[0:64, 0:64], Hz[:, j, 0, :], ident[:, 0:64])
                mm(pb_[0:64, 64:128], Hz[:, j, 1, :], ident[:, 64:128])
                cp(so[0:64, j * 64:(j + 1) * 64], pb_[0:64, 0:64])
                cp(stmp[0:64, 0:64], pb_[0:64, 64:128], eng="act")
                mm(pb_[:, 128:192], shiftI[:, :], stmp[:, 0:64])
                cp(so[64:128, j * 64:(j + 1) * 64], pb_[64:128, 128:192])
            dma_out(o_wkv_p.rearrange("j p k -> p j k"), so[:, 0:384].rearrange("p (j k) -> p j k", j=6))
            free([so])

    def load_xT(xsrc, NBK):
        xT = alloc(16)
        for blk in range(NBK):
            xt = alloc(4)
            for q in range(4):
                dma_in(xt[q][:], xsrc[blk * 128:(blk + 1) * 128, q * 512:(q + 1) * 512])
            for c in range(16):
                pb_ = banks[2 + (c % 2)]
                tr(pb_[:, 0:128], xt[c // 4][:, (c % 4) * 128:(c % 4 + 1) * 128])
                cp(xT[c][:, blk * 128:(blk + 1) * 128], pb_[:, 0:128], eng=("act" if c % 2 else "dve"))
            free(xt)
        return xT

    def tile_stage(kind, ti, mode="full"):
        prompt = kind == "p"
        NT = TT if prompt else 128
        NBK = NT // 128
        CL = 64 if prompt else 8
        NCH = NT // CL
        NSEQ = 1 if prompt else 16
        SL = NT // NSEQ
        xsrc = xp[ti * TT:(ti + 1) * TT, :] if prompt else xs
        last_p = prompt and ti == n_ptiles - 1
        d0 = (ti == 0)

        xT = load_xT(xsrc, NBK)
        rs = rms_rstd(xT, NT)
        xn = ralloc(16)
        for c in range(16):
            stt(xn[c][:, 0:NT], xT[c][:, 0:NT], C("g_pre_mix", c), rs[:, 0:NT], ALU.mult, ALU.mult)
        free([rs])
        free(xT)

        if mode == "rwA":
            orwb = ralloc(6)
            pT = alloc(20)
            for n in range(20):
                gemm_fm(("rw", n), xn, NT, lambda ps, n=n: cp(pT[n][:, 0:NT], ps, eng="act"))
            rwkv_stage(kind, ti, pT, orwb, NT, NSEQ, SL, CL, NCH, prompt, last_p, o_from=NCH)
            free(pT)
            rfree(orwb)
            rfree(xn)
            return

        q = alloc(6)
        kd = alloc(4)
        vt = alloc(2)
        qm = alloc(4)
        osw = ralloc(6)
        omem = ralloc(4)
        orwb = ralloc(6)
        for n in range(6):
            gemm_fm(("q", n), xn, NT, lambda ps, n=n: cp(q[n][:, 0:NT], ps, eng="act"))
        for h in range(4):
            gemm_fm(("kd", h), xn, NT, lambda ps, h=h: cp(kd[h][:, 0:NT], ps, eng="act"))
        for n in range(2):
            s, kc = wload(("v", n))
            pb_ = gbank()
            for blk in range(NBK):
                for k in range(16):
                    mm(pb_[:, blk * 128:(blk + 1) * 128], xn[k][:, blk * 128:(blk + 1) * 128].r,
                       s[:, k * 128:(k + 1) * 128], start=(k == 0), stop=(k == 15))
            cp(vt[n][:, 0:NT], pb_[:, 0:NT], eng="act")
        for n in range(4):
            gemm_fm(("qm", n), xn, NT, lambda ps, n=n: cp(qm[n][:, 0:NT].r, ps, eng="act"))
        if d0:
            dbg_out("q" + kind, q[1][:, 0:NT])
            dbg_out("vt" + kind, vt[1][:, 0:NT])

        def Vh(blk, hk):
            return vt[hk // 2][:, blk * 128 + (hk % 2) * 64: blk * 128 + (hk % 2) * 64 + 64]

        if stage >= 2:
            if prompt:
                for qb in (range(NBK) if mode != "mixB" else [NBK - 1]):
                    qs = slice(qb * 128, (qb + 1) * 128)
                    gblk = ti * NBK + qb
                    kbl = []
                    if gblk > 0:
                        if qb == 0:
                            kbl.append(("P", lambda hk, par: kdprev[par * 64:(par + 1) * 64, hk, :],
                                        lambda hk: vprev[:, hk * 64:(hk + 1) * 64]))
                        else:
                            kbl.append(("P", lambda hk, par, qb=qb: kd[hk][par * 64:(par + 1) * 64, (qb - 1) * 128:qb * 128],
                                        lambda hk, qb=qb: Vh(qb - 1, hk)))
                    kbl.append(("D", lambda hk, par, qb=qb: kd[hk][par * 64:(par + 1) * 64, qb * 128:(qb + 1) * 128],
                                lambda hk, qb=qb: Vh(qb, hk)))
                    nkb = len(kbl)
                    for cset in range(2):
                        ob, db = banks[2], banks[3]
                        for par in range(2):
                            heads = [2 * (3 * cset + k) + par for k in range(3)]
                            Es = alloc(nkb)
                            for kbi, (typ, kf, vf) in enumerate(kbl):
                                sbk = banks[4 + 2 * kbi + par]
                                for k, i in enumerate(heads):
                                    mm(sbk[:, k * 128:(k + 1) * 128], kf(i // 3, par),
                                       q[i // 2][par * 64:(par + 1) * 64, qs])
                                act(Es[kbi][:, 0:384], sbk[:, 0:384], AF.Exp, scale=0.125)
                                tt(Es[kbi][:, 0:384], Es[kbi][:, 0:384], (maskP4 if typ == "P" else maskD4)[:, 0:384],
                                   ALU.mult, eng="dve")
                                if typ == "P" and qb == 0 and ti == 2:
                                    ts(Es[kbi][:, 0:384], Es[kbi][:, 0:384], flagt[:, 0:1], None, ALU.mult)
                            for k, i in enumerate(heads):
                                rp, rc = slice(par * 64, (par + 1) * 64), slice(k * 128, (k + 1) * 128)
                                for kbi, (typ, kf, vf) in enumerate(kbl):
                                    mm(ob[rp, rc], vf(i // 3), Es[kbi][:, k * 128:(k + 1) * 128],
                                       start=(kbi == 0), stop=(kbi == nkb - 1))
                                for kbi, (typ, kf, vf) in enumerate(kbl):
                                    mm(db[rp, rc], ones[:, 0:64], Es[kbi][:, k * 128:(k + 1) * 128],
                                       start=(kbi == 0), stop=(kbi == nkb - 1))
                            free(Es)
                        tmp = alloc(1)[0]
                        for k in range(3):
                            ci = 3 * cset + k
                            act(tmp[:, k * 128:(k + 1) * 128], db[:, k * 128:(k + 1) * 128], AF.Identity,
                                bias=esink[:, ci:ci + 1])
                        recip(tmp[:, 0:384], tmp[:, 0:384])
                        for k in range(3):
                            ci = 3 * cset + k
                            tt(osw[ci][:, qs], ob[:, k * 128:(k + 1) * 128], tmp[:, k * 128:(k + 1) * 128], ALU.mult)
                        free([tmp])
                for hk in range(4):
                    cp(kdprev[:, hk, :], kd[hk][:, NT - 128:NT], eng="dve")
                for n in range(2):
                    cp(vprev[:, n * 128:(n + 1) * 128], vt[n][:, NT - 128:NT], eng="dve")
                if last_p:
                    pb_ = banks[2]
                    for hk in range(4):
                        tr(pb_[:, hk * 64:(hk + 1) * 64], kd[hk][0:64, NT - 128:NT])
                    kt = alloc(1)[0]
                    cp(kt[:, 0:256], pb_[:, 0:256])
                    dma_out(o_swk_p[:, :], kt[:, 0:256])
                    dma_out(o_swv_p[:, :], vprev[:, :])
                    free([kt])
            else:
                P.dma("sp", lambda e: e.dma_start(out=o_swk_s[:, 0:120, :], in_=csk[:, 8:128, :]), (), [dummy], final=True)
                P.dma("sp", lambda e: e.dma_start(out=o_swv_s[:, 0:120, :], in_=csv[:, 8:128, :]), (), [dummy2], final=True)
                pb_ = banks[2]
                for hk in range(4):
                    tr(pb_[:, hk * 64:(hk + 1) * 64], kd[hk][0:64, 0:128])
                kt = alloc(1)[0]
                cp(kt[:, 0:256], pb_[:, 0:256])
                cp(kt[:, 256:384], vt[0][:, 0:128])
                cp(kt[:, 384:512], vt[1][:, 0:128])
                for b in range(16):
                    dma_out(o_swk_s[b, 120:128, :], kt[b * 8:(b + 1) * 8, 0:256])
                    dma_out(o_swv_s[b, 120:128, :], kt[b * 8:(b + 1) * 8, 256:512])
                free([kt])
                OC = [banks[5], banks[6]]
                DEN = [banks[0], banks[1]]
                SB2 = [banks[4], banks[3]]
                kcd = alloc(2)
                vcb = alloc(2)
                kcT = alloc(2)
                Eb = alloc(2)
                for b in range(16):
                    kc_ = kcd[b % 2]
                    vc_ = vcb[b % 2]
                    kv = kc_[:, :].rearrange("p (h u d) -> p h u d", h=4, u=2)
                    src = csk[b].rearrange("p (h d) -> p h d", h=4)
                    dma_in(kv[:, :, 0, :], src)
                    dma_in(kv[:, :, 1, :], src)
                    dma_in(vc_[:, 0:256], csv[b])
                    pb_ = banks[2]
                    for hk in range(4):
                        tr(pb_[:, hk * 128:(hk + 1) * 128], kc_[:, hk * 128:(hk + 1) * 128])
                    kT = kcT[b % 2]
                    cp(kT[:], pb_[:], eng="act")
                    for par in range(2):
                        for k in range(6):
                            i = 2 * k + par
                            hk = i // 3
                            mm(SB2[par][:, k * 8:(k + 1) * 8], kT[par * 64:(par + 1) * 64, hk * 128:(hk + 1) * 128],
                               q[k][par * 64:(par + 1) * 64, b * 8:(b + 1) * 8])
                    E_ = Eb[b % 2]
                    for par in range(2):
                        act(E_[:, par * 48:(par + 1) * 48], SB2[par][:, 0:48], AF.Exp, scale=0.125)
                    tt(E_[:, 0:96], E_[:, 0:96], maskC96[:, :], ALU.mult)
                    for i in range(12):
                        par = i % 2
                        hk = i // 3
                        ci = i // 2
                        col = (ci % 4) * 128 + b * 8
                        ec = par * 48 + ci * 8
                        mm(OC[ci // 4][par * 64:(par + 1) * 64, col:col + 8], vc_[:, hk * 64:(hk + 1) * 64],
                           E_[:, ec:ec + 8])
                        mm(DEN[ci // 4][par * 64:(par + 1) * 64, col:col + 8], ones[:, 0:64], E_[:, ec:ec + 8])
                free(kcd + vcb + kcT + Eb)
                for cset in range(2):
                    onb, dnb = banks[7], banks[2]
                    for par in range(2):
                        heads = [2 * (3 * cset + k) + par for k in range(3)]
                        sbk = SB2[par]
                        for k, i in enumerate(heads):
                            mm(sbk[:, k * 128:(k + 1) * 128], kd[i // 3][par * 64:(par + 1) * 64, 0:128],
                               q[i // 2][par * 64:(par + 1) * 64, 0:128])
                        En = alloc(1)[0]
                        act(En[:, 0:384], sbk[:, 0:384], AF.Exp, scale=0.125)
                        tt(En[:, 0:384], En[:, 0:384], maskN4[:, 0:384], ALU.mult)
                        for k, i in enumerate(heads):
                            rp, rc = slice(par * 64, (par + 1) * 64), slice(k * 128, (k + 1) * 128)
                            mm(onb[rp, rc], Vh(0, i // 3), En[:, k * 128:(k + 1) * 128])
                            mm(dnb[rp, rc], ones[:, 0:64], En[:, k * 128:(k + 1) * 128])
                        free([En])
                    ons, dns, tmp, tmp2 = alloc(4)
                    cp(ons[:, 0:384], onb[:, 0:384], eng="act")
                    cp(dns[:, 0:384], dnb[:, 0:384], eng="act")
                    for k in range(3):
                        ci = 3 * cset + k
                        csl = slice((ci % 4) * 128, (ci % 4 + 1) * 128)
                        ks = slice(k * 128, (k + 1) * 128)
                        tt(tmp[:, ks], DEN[ci // 4][:, csl], dns[:, ks], ALU.add)
                        act(tmp[:, ks], tmp[:, ks], AF.Identity, bias=esink[:, ci:ci + 1])
                        tt(tmp2[:, ks], OC[ci // 4][:, csl], ons[:, ks], ALU.add)
                    recip(tmp[:, 0:384], tmp[:, 0:384])
                    for k in range(3):
                        ci = 3 * cset + k
                        ks = slice(k * 128, (k + 1) * 128)
                        tt(osw[ci][:, 0:128], tmp2[:, ks], tmp[:, ks], ALU.mult)
                    free([ons, dns, tmp, tmp2])
            if prompt:
                for h in range(4):
                    Es = alloc(2)
                    for mb in range(2):
                        sbk = banks[4 + mb]
                        mm(sbk[:, 0:NT], memKT[:, h, mb * 128:(mb + 1) * 128].r, qm[h][:, 0:NT].r)
                        act(Es[mb][:, 0:NT].r, sbk[:, 0:NT], AF.Exp, scale=128 ** -0.5)
                    ob, db = banks[6], banks[7]
                    for mb in range(2):
                        mm(ob[:, 0:NT], memV[:, mb, h * 128:(h + 1) * 128].r, Es[mb][:, 0:NT].r, start=(mb == 0), stop=(mb == 1))
                    for mb in range(2):
                        mm(db[:, 0:NT], ones[:], Es[mb][:, 0:NT], start=(mb == 0), stop=(mb == 1))
                    free(Es)
                    tmp = alloc(1)[0]
                    recip(tmp[:, 0:NT], db[:, 0:NT])
                    tt(omem[h][:, 0:NT], ob[:, 0:NT], tmp[:, 0:NT], ALU.mult)
                    free([tmp])
            else:
                OM, DM = banks[6], banks[7]
                kmb = alloc(4)
                vmb = alloc(4)
                kmT = alloc(4)
                Eb = alloc(2)
                for b in range(16):
                    pp = (b % 2) * 2
                    for mb in range(2):
                        dma_in(kmb[pp + mb][:], cmk[b, mb * 128:(mb + 1) * 128, :])
                        dma_in(vmb[pp + mb][:], cmv[b, mb * 128:(mb + 1) * 128, :])
                        pb_ = banks[3 if mb else 2]
                        for h in range(4):
                            tr(pb_[:, h * 128:(h + 1) * 128], kmb[pp + mb][:, h * 128:(h + 1) * 128])
                        cp(kmT[pp + mb][:], pb_[:], eng=("act" if mb else "dve"))
                    sbk = banks[4]
                    for mb in range(2):
                        for h in range(4):
                            mm(sbk[:, (mb * 4 + h) * 8:(mb * 4 + h + 1) * 8], kmT[pp + mb][:, h * 128:(h + 1) * 128],
                               qm[h][:, b * 8:(b + 1) * 8])
                    E_ = Eb[b % 2]
                    act(E_[:, 0:64], sbk[:, 0:64], AF.Exp, scale=128 ** -0.5)
                    for h in range(4):
                        for mb in range(2):
                            mm(OM[:, h * 128 + b * 8:h * 128 + (b + 1) * 8], vmb[pp + mb][:, h * 128:(h + 1) * 128],
                               E_[:, (mb * 4 + h) * 8:(mb * 4 + h + 1) * 8], start=(mb == 0), stop=(mb == 1))
                        for mb in range(2):
                            mm(DM[:, h * 128 + b * 8:h * 128 + (b + 1) * 8], ones[:],
                               E_[:, (mb * 4 + h) * 8:(mb * 4 + h + 1) * 8], start=(mb == 0), stop=(mb == 1))
                free(kmb + vmb + kmT + Eb)
                tmp = alloc(1)[0]
                recip(tmp[:], DM[:])
                for h in range(4):
                    tt(omem[h][:, 0:128], OM[:, h * 128:(h + 1) * 128], tmp[:, h * 128:(h + 1) * 128], ALU.mult)
                free([tmp])
        free(kd + vt)
        free(q + qm)

        pT = alloc(20)
        for n in range(20):
            gemm_fm(("rw", n), xn, NT, lambda ps, n=n: cp(pT[n][:, 0:NT], ps, eng="act"))
        if d0:
            dbg_out("prw" + kind, pT[7][:, 0:NT])
        if stage >= 3:
            rwkv_stage(kind, ti, pT, orwb, NT, NSEQ, SL, CL, NCH, prompt, last_p,
                       o_from=(NCH - 2 if mode == "mixB" else 0))
        free(pT)

        NE = 128 if mode == "mixB" else NT
        CE = NT - NE
        merged = ralloc(16)
        obr = [orwb, osw, omem]
        for n in range(16):
            gsb = alloc(3)
            for i in range(3):
                gemm_fm(("g", i, n), xn, NE, lambda ps, i=i: act(gsb[i][:, 0:NE], ps, AF.Sigmoid), col0=CE)
                if i == 0:
                    gemm_fm(("br", i, n), obr[i], NE,
                            lambda ps, n=n: tt(gsb[0][:, 0:NE], ps, gsb[0][:, 0:NE], ALU.mult), col0=CE)
                else:
                    def ev(ps, i=i, n=n):
                        tt(gsb[i][:, 0:NE], ps, gsb[i][:, 0:NE], ALU.mult)
                        if i == 1:
                            tt(gsb[0][:, 0:NE], gsb[0][:, 0:NE], gsb[1][:, 0:NE], ALU.add)
                        else:
                            tt(merged[n][:, 0:NE], gsb[0][:, 0:NE], gsb[2][:, 0:NE], ALU.add)
                    gemm_fm(("br", i, n), obr[i], NE, ev, col0=CE)
            free(gsb)
        rfree(xn)
        rfree(orwb + osw + omem)

        post = alloc(16)
        for n in range(16):
            gemm_fm(("o", n), merged, NE, lambda ps, n=n: cp(post[n][:, 0:NE], ps, eng="act"))
        rfree(merged)
        rs = rms_rstd(post, NE)
        hT = load_xT(xsrc[CE:NT, :], NE // 128)
        for n in range(16):
            stt(post[n][:, 0:NE], post[n][:, 0:NE], C("g_post_mix", n), rs[:, 0:NE], ALU.mult, ALU.mult)
            tt(hT[n][:, 0:NE], hT[n][:, 0:NE], post[n][:, 0:NE], ALU.add, eng="dve")
        free([rs])
        free(post)
        if d0:
            dbg_out("h" + kind, hT[5][:, 0:NE])
        rs = rms_rstd(hT, NE)
        hn = ralloc(16)
        for n in range(16):
            stt(hn[n][:, 0:NE], hT[n][:, 0:NE], C("g_pre_ffn", n), rs[:, 0:NE], ALU.mult, ALU.mult)
        free([rs])

        if mode == "mixB":
            for j in range(NFF):
                gemm_fm(("ug", j), hn, NE, lambda ps, j=j: cp(czc[:, j, :], ps[:, NE - 2:NE], eng="act"))
            ts(czc[:, :, :].rearrange("p j t -> p (j t)"), czc[:, :, :].rearrange("p j t -> p (j t)"), flagt[:, 0:1], None,
               ALU.mult)
            rfree(hn)
            free(hT)
            return

        yacc = alloc(16)
        cw, cb_ = CI["conv_w"][0], CI["conv_b"][0]
        NG, GS = 4, 11
        W2 = SL + 2
        fbuf = [ralloc(GS), ralloc(GS)]
        GCH = {}
        stT_cache = {}

        def sample_state(grp4):
            st_in = alloc(1)[0]
            dma_in(st_in[0:32, :], sconv[:, grp4 * 512:(grp4 + 1) * 512])
            pb_ = banks[2]
            for jj in range(4):
                tr(pb_[:, jj * 32:(jj + 1) * 32], st_in[0:32, jj * 128:(jj + 1) * 128])
            stT = alloc(1)[0]
            cp(stT[:, 0:128], pb_[:, 0:128])
            free([st_in])
            return stT

        GST = {}

        def GATE_MM(j):
            w2c = cst[:, cw + 2 * NFF + j:cw + 2 * NFF + j + 1]
            bc = cst[:, cb_ + j:cb_ + j + 1]
            zc, c_, u_ = alloc(3)
            GST[j] = (zc, c_, u_)
            if prompt:
                def ev(ps):
                    cp(zc[:, 0:NT], ps, eng="act")
                    act(c_[:, 0:NT], ps, AF.Identity, bias=bc, scale=w2c)
            else:
                zall = zc[:, 0:NSEQ * W2].rearrange("p (b t) -> p b t", b=NSEQ)

                def ev(ps):
                    cp(zall[:, :, 2:W2], ps.rearrange("p (b t) -> p b t", b=NSEQ), eng="act")
                    act(c_[:, 0:NT], ps, AF.Identity, bias=bc, scale=w2c)
            gemm_fm(("ug", j), hn, NT, ev)

        def CONV_OUT(jo):
            if jo < 0:
                return
            pb_ = banks[2]
            if prompt:
                tr(pb_[0:2, 0:128], czc[:, jo, :])
                cp(cout[0:2, (jo % 2) * 128:(jo % 2 + 1) * 128], pb_[0:2, 0:128], eng="act")
                dma_out(o_conv_p[:, jo * 128:(jo + 1) * 128], cout[0:2, (jo % 2) * 128:(jo % 2 + 1) * 128])
            else:
                tr(pb_[0:32, 128:256], cstg[:, (jo % 4) * 32:(jo % 4 + 1) * 32])
                cp(cout[0:32, (jo % 2) * 128:(jo % 2 + 1) * 128], pb_[0:32, 128:256], eng="act")
                dma_out(o_conv_s[:, jo * 128:(jo + 1) * 128], cout[0:32, (jo % 2) * 128:(jo % 2 + 1) * 128])

        def GATE_CHAIN(j):
            w0c = cst[:, cw + j:cw + j + 1]
            w1c = cst[:, cw + NFF + j:cw + NFF + j + 1]
            w2c = cst[:, cw + 2 * NFF + j:cw + 2 * NFF + j + 1]
            bc = cst[:, cb_ + j:cb_ + j + 1]
            zc, c_, u_ = GST.pop(j)
            if prompt:
                zp = zpb[:, (j % 2) * 4:(j % 2) * 4 + 4]
                cp(zp[:, 0:2], czc[:, j, :])
                stt(c_[:, 1:NT], zc[:, 0:NT - 1], w1c, c_[:, 1:NT], ALU.mult, ALU.add)
                stt(c_[:, 0:1], zp[:, 1:2], w1c, c_[:, 0:1], ALU.mult, ALU.add)
                stt(c_[:, 2:NT], zc[:, 0:NT - 2], w0c, c_[:, 2:NT], ALU.mult, ALU.add)
                stt(c_[:, 0:2], zp[:, 0:2], w0c, c_[:, 0:2], ALU.mult, ALU.add)
                cp(czc[:, j, :], zc[:, NT - 2:NT])
                if last_p:
                    CONV_OUT(j - 2)
                    if j == NFF - 1:
                        CONV_OUT(j - 1)
                        CONV_OUT(j)
            else:
                if j % 4 == 0:
                    stT_cache["cur"] = sample_state(j // 4)
                stT = stT_cache["cur"]
                jj = j % 4
                zall = zc[:, 0:NSEQ * W2].rearrange("p (b t) -> p b t", b=NSEQ)
                cp(zall[:, :, 0:2], stT[:, jj * 32:(jj + 1) * 32].rearrange("p (b t) -> p b t", b=NSEQ))
                cv = c_[:, 0:NT].rearrange("p (b t) -> p b t", b=NSEQ)
                stt(cv, zall[:, :, 1:W2 - 1], w1c, cv, ALU.mult, ALU.add)
                stt(cv, zall[:, :, 0:W2 - 2], w0c, cv, ALU.mult, ALU.add)
                cp(cstg[:, (j % 4) * 32:(j % 4 + 1) * 32].rearrange("p (b t) -> p b t", b=NSEQ), zall[:, :, SL:SL + 2])
                CONV_OUT(j - 2)
                if j == NFF - 1:
                    CONV_OUT(j - 1)
                    CONV_OUT(j)
                if jj == 3:
                    free([stT])
            act(u_[:, 0:NT], c_[:, 0:NT], AF.Square)
            stt(u_[:, 0:NT], u_[:, 0:NT], 22.363860002236387, c_[:, 0:NT], ALU.add, ALU.mult)
            act(u_[:, 0:NT], u_[:, 0:NT], AF.Sigmoid, scale=1.5957691216057308 * 0.044715)
            tt(c_[:, 0:NT], c_[:, 0:NT], u_[:, 0:NT], ALU.mult)
            free([zc, u_])
            GCH[j] = c_

        def VAL(j):
            c_ = GCH.pop(j)
            fb = fbuf[(j // GS) % 2][j % GS]
            gemm_fm(("uv", j), hn, NT, lambda ps: tt(fb[:, 0:NT], ps, c_[:, 0:NT], ALU.mult))
            free([c_])

        def DOWN(g):
            fg = fbuf[g % 2]
            for n in range(16):
                if g == 0:
                    gemm_fm(("dn", g, n), fg, NT, lambda ps, n=n: cp(yacc[n][:, 0:NT], ps, eng="act"))
                else:
                    gemm_fm(("dn", g, n), fg, NT, lambda ps, n=n: tt(yacc[n][:, 0:NT], yacc[n][:, 0:NT], ps, ALU.add))

        GATE_MM(0)
        GATE_CHAIN(0)
        pending_down = None
        for j in range(NFF):
            if j + 1 < NFF:
                GATE_MM(j + 1)
            VAL(j)
            if j + 1 < NFF:
                GATE_CHAIN(j + 1)
            if pending_down is not None and (j % GS) == 2:
                DOWN(pending_down)
                pending_down = None
            if (j + 1) % GS == 0:
                pending_down = j // GS
        DOWN(pending_down)
        rfree(fbuf[0] + fbuf[1])
        rfree(hn)
        rs = rms_rstd(yacc, NT)
        for n in range(16):
            stt(yacc[n][:, 0:NT], yacc[n][:, 0:NT], C("g_post_ffn", n), rs[:, 0:NT], ALU.mult, ALU.mult)
            tt(yacc[n][:, 0:NT], yacc[n][:, 0:NT], hT[n][:, 0:NT], ALU.add, eng="dve")
        free([rs])
        free(hT)
        yoff = (ti - 2) if n_ptiles == 4 else ti
        ydst = yp[yoff * TT:(yoff + 1) * TT, :] if prompt else ys
        for blk in range(NBK):
            for qd in range(4):
                pb_ = banks[2 + qd % 2]
                for cc in range(4):
                    n = qd * 4 + cc
                    tr(pb_[:, cc * 128:(cc + 1) * 128], yacc[n][:, blk * 128:(blk + 1) * 128])
                yo = alloc(1)[0]
                cp(yo[:], pb_[:], eng=("act" if qd % 2 else "dve"))
                dma_out(ydst[blk * 128:(blk + 1) * 128, qd * 512:(qd + 1) * 512], yo[:])
                free([yo])
        free(yacc)

    if n_ptiles > 0:
        mem_kv_stage()
    if USE_SCR and PROLOGUE:
        for key in list(L.idx.keys()):
            if key[0] == "mkv":
                continue
            off, n = L.idx[key]
            s_ = wslots[wctr[0] % NSLOT]
            wh_ = whs[wctr[0] % NSLOT]
            wctr[0] += 1
            P.dma("pool", lambda e, s_=s_, off=off, n=n: e.dma_start(out=s_.t[:, 0:n], in_=wall[:, off:off + n]), (), [s_])
            o_ = P.dma("sp", lambda e, s_=s_, off=off, n=n: e.dma_start(out=wscr[:, off:off + n], in_=s_.t[:, 0:n]), [s_], (),
                       sem_buf=wh_)
            scr_ev[key] = o_.ev
    modes = ["rwA", "mixB", "full", "full"]
    for ti in range(n_ptiles):
        tile_stage("p", ti, modes[ti] if n_ptiles == 4 else "full")
    if do_sample:
        tile_stage("s", 0)

    P.emit()
    es.close()
    return nc


_CACHE = {}


def _prep_inputs(inp):
    L, wall = build_wall(inp)
    consts = build_consts(inp)
    in_maps = []
    for c in range(8):
        pb = c % 4
        half = c // 4
        bs = slice(16 * c, 16 * (c + 1))
        if half == 0:
            xpc = np.concatenate([np.zeros((SEQ // 2, D), np.float32), inp["x_prompt"][pb][:SEQ // 2]], axis=0)
        else:
            xpc = inp["x_prompt"][pb]
        m = {
            "xp": np.ascontiguousarray(xpc),
            "flag": np.full((128, 1), float(half), np.float32),
            "xs": np.ascontiguousarray(inp["x_sample"][bs].reshape(128, D)),
            "memp": np.ascontiguousarray(inp["mem_prompt"][pb]),
            "csk": np.ascontiguousarray(inp["cache_swa_k"][bs].reshape(16, 128, 256)),
            "csv": np.ascontiguousarray(inp["cache_swa_v"][bs].reshape(16, 128, 256)),
            "cmk": np.ascontiguousarray(inp["cache_mem_k"][bs].reshape(16, 256, 512)),
            "cmv": np.ascontiguousarray(inp["cache_mem_v"][bs].reshape(16, 256, 512)),
            "sshift": np.ascontiguousarray(inp["state_rwkv_shift"][bs]),
            "swkv": np.ascontiguousarray(inp["state_rwkv_wkv"][bs].reshape(16, 6, 128, 64)),
            "sconv": np.ascontiguousarray(inp["state_ffn_conv"][bs].reshape(32, DFF)),
            "wall": wall,
            "consts": consts,
        }
        in_maps.append(m)
    return L, wall.shape[1], consts.shape[1], in_maps


def kernel(**inp):
    inp = {k: np.asarray(v, np.float32) for k, v in inp.items()}
    L, ncw, ncc, in_maps = _prep_inputs(inp)
    key = (ncw, ncc)
    if key not in _CACHE:
        _CACHE[key] = build_program(L, ncw, ncc)
    nc = _CACHE[key]
    res = run_bass_kernel_spmd(nc, in_maps, core_ids=list(range(8)))
    R = res.results
    y_p = np.stack([np.concatenate([R[b]["yp"], R[b + 4]["yp"]], axis=0) for b in range(4)])
    y_s = np.concatenate([R[c]["ys"].reshape(16, 8, D) for c in range(8)])
    swk_p = np.stack([R[b + 4]["o_swk_p"].reshape(128, 4, 64) for b in range(4)])
    swv_p = np.stack([R[b + 4]["o_swv_p"].reshape(128, 4, 64) for b in range(4)])
    mk_p = np.stack([R[b]["o_mk_p"].reshape(256, 4, 128) for b in range(4)])
    mv_p = np.stack([R[b]["o_mv_p"].reshape(256, 4, 128) for b in range(4)])
    sh_p = np.stack([R[b + 4]["o_shift_p"].reshape(RWP) for b in range(4)])
    wkv_p = np.stack([R[b + 4]["o_wkv_p"].reshape(12, 64, 64) for b in range(4)])
    cv_p = np.stack([R[b + 4]["o_conv_p"] for b in range(4)])
    swk_s = np.concatenate([R[c]["o_swk_s"].reshape(16, 128, 4, 64) for c in range(8)])
    swv_s = np.concatenate([R[c]["o_swv_s"].reshape(16, 128, 4, 64) for c in range(8)])
    sh_s = np.concatenate([R[c]["o_shift_s"] for c in range(8)])
    wkv_s = np.concatenate([R[c]["o_wkv_s"].reshape(16, 12, 64, 64) for c in range(8)])
    cv_s = np.concatenate([R[c]["o_conv_s"].reshape(16, 2, DFF) for c in range(8)])
    return (y_p, y_s, swk_p, swv_p, mk_p, mv_p, sh_p, wkv_p, cv_p, swk_s, swv_s, sh_s, wkv_s, cv_s)
```

```python
import contextlib
import os
SUB = os.environ.get('SUB', 'abc')
import numpy as np
import concourse.bass as bass
import concourse.mybir as mybir
from concourse.bass_utils import run_bass_kernel_spmd

F32 = mybir.dt.float32
F32R = mybir.dt.float32r
BF16 = mybir.dt.bfloat16
AF = mybir.ActivationFunctionType
ALU = mybir.AluOpType
AX = mybir.AxisListType
ENGS = ("pe", "act", "dve", "pool", "sp")

D = 2048
SEQ = 2048
RWP = 2560
DFF = 5632
NFF = 44
TT = 512
EPS = 1e-6
USE_R = False
USE_SCR = True
PROLOGUE = False
HADD = int(os.environ.get('HADD', '1'))


class Buf:
    __slots__ = ("name", "t", "last_w", "reads", "dsem", "dcount", "const", "psum")

    def __init__(self, name, t):
        self.name = name
        self.t = t
        self.last_w = None
        self.reads = []
        self.dsem = None
        self.dcount = 0
        self.const = False
        self.psum = False

    def __getitem__(self, idx):
        return View(self, self.t[idx])


class View:
    __slots__ = ("buf", "ap")

    def __init__(self, buf, ap):
        self.buf = buf
        self.ap = ap

    def bitcast(self, dt):
        return View(self.buf, self.ap.bitcast(dt))

    def rearrange(self, pat, **kw):
        return View(self.buf, self.ap.rearrange(pat, **kw))

    def __getitem__(self, idx):
        return View(self.buf, self.ap[idx])

    @property
    def r(self):
        if not USE_R:
            return self
        return View(self.buf, self.ap.bitcast(F32R))


def _ap(x):
    return x.ap if isinstance(x, View) else x


def _bufs(*xs):
    out = []
    for x in xs:
        if isinstance(x, View) and x.buf not in out:
            out.append(x.buf)
    return out


class Op:
    __slots__ = ("eng", "fn", "waits", "is_dma", "dsem", "dval", "sig", "needs_inc", "ev")


class Prog:
    def __init__(self, nc):
        self.nc = nc
        self.ops = {e: [] for e in ENGS}
        self.all_ops = []
        self.final_events = []

    def _deps(self, reads, writes):
        deps = []
        for b in reads:
            if b.last_w is not None:
                deps.append(b.last_w)
        for b in writes:
            if b.last_w is not None:
                deps.append(b.last_w)
            deps.extend(b.reads)
        return deps

    def _commit(self, o, ev, reads, writes):
        self.ops[o.eng].append(o)
        self.all_ops.append(o)
        for b in writes:
            b.last_w = ev
            b.reads = []
        for b in reads:
            if b not in writes and not b.const:
                b.reads.append(ev)

    def op(self, eng, fn, reads=(), writes=()):
        pr = [b for b in reads if b.psum]
        if pr:
            reads = [b for b in reads if not b.psum]
            writes = list(writes) + [b for b in pr if b not in writes]
        o = Op()
        o.eng, o.fn, o.is_dma = eng, fn, False
        o.waits = self._deps(reads, writes)
        o.needs_inc, o.sig = False, None
        self._commit(o, ("op", o), reads, writes)
        return o

    def dma(self, queue, fn, reads=(), writes=(), final=False, sem_buf=None, extra=()):
        o = Op()
        o.eng, o.fn, o.is_dma = queue, fn, True
        o.waits = self._deps(reads, writes) + list(extra)
        o.needs_inc, o.sig = True, None
        sb = sem_buf if sem_buf is not None else (writes[0] if writes else reads[0])
        if sb.dsem is None:
            sb.dsem = self.nc.alloc_semaphore(name=("d_" + sb.name)[:40])
        sb.dcount += 1
        o.dsem, o.dval = sb.dsem, 16 * sb.dcount
        ev = ("dma", o)
        self._commit(o, ev, reads, writes)
        if final:
            self.final_events.append(ev)
        o.ev = ev
        return o

    def emit(self):
        nc = self.nc
        for o in self.all_ops:
            for kind, d in o.waits:
                if kind == "op" and not (d.eng == o.eng and d.eng == "pe"):
                    d.needs_inc = True
        esem = {}
        for e in ENGS:
            cnt = 0
            for o in self.ops[e]:
                if not o.is_dma and o.needs_inc:
                    cnt += 1
                    o.sig = cnt
            if cnt:
                esem[e] = nc.alloc_semaphore(name="e_" + e)
        finals = self.final_events

        def run(e, eng):
            known = {}

            def wait_all(evs):
                need = {}
                for kind, d in evs:
                    if kind == "op":
                        if d.eng == e and e == "pe":
                            continue
                        s, v = esem[d.eng], d.sig
                    else:
                        s, v = d.dsem, d.dval
                    if known.get(s.num, 0) >= v:
                        continue
                    if s.num not in need or need[s.num][1] < v:
                        need[s.num] = (s, v)
                for k, (s, v) in need.items():
                    eng.wait_ge(s, v)
                    known[k] = v

            for o in self.ops[e]:
                wait_all(o.waits)
                inst = o.fn(eng)
                if o.is_dma:
                    inst.then_inc(o.dsem, 16)
                elif o.needs_inc:
                    inst.then_inc(esem[e], 1)
            if e == "sp":
                wait_all(finals)

        with nc.Block() as block:
            @block.tensor
            def _(eng):
                run("pe", eng)

            @block.scalar
            def _(eng):
                run("act", eng)

            @block.vector
            def _(eng):
                run("dve", eng)

            @block.gpsimd
            def _(eng):
                run("pool", eng)

            @block.sync
            def _(eng):
                run("sp", eng)


def _blk(w, c0, c1):
    K = w.shape[0]
    kc = K // 128
    sub = w[:, c0:c1]
    return np.ascontiguousarray(sub.reshape(kc, 128, c1 - c0).transpose(1, 0, 2).reshape(128, kc * (c1 - c0)))


def _pcol(v, n):
    return np.ascontiguousarray(np.asarray(v, np.float32).reshape(n, 128).T)


class WLayout:
    def __init__(self):
        self.cols = 0
        self.parts = []
        self.idx = {}

    def add(self, key, arr):
        self.idx[key] = (self.cols, arr.shape[1])
        self.parts.append(arr)
        self.cols += arr.shape[1]


def build_wall(inp):
    L = WLayout()
    w_in = inp["w_in"]
    o_q = RWP
    o_k = o_q + 768
    o_v = o_k + 256
    o_m = o_v + 256
    o_g = o_m + 512
    for n in range(20):
        L.add(("rw", n), _blk(w_in, n * 128, (n + 1) * 128))
    for n in range(6):
        L.add(("q", n), _blk(w_in, o_q + n * 128, o_q + (n + 1) * 128))
    for h in range(4):
        kb = w_in[:, o_k + h * 64:o_k + (h + 1) * 64]
        L.add(("kd", h), _blk(np.concatenate([kb, kb], axis=1), 0, 128))
    for n in range(2):
        L.add(("v", n), _blk(w_in, o_v + n * 128, o_v + (n + 1) * 128))
    for n in range(4):
        L.add(("qm", n), _blk(w_in, o_m + n * 128, o_m + (n + 1) * 128))
    br = [inp["w_br_rwkv"], inp["w_br_swa"], inp["w_br_mem"]]
    for n in range(16):
        for i in range(3):
            L.add(("g", i, n), _blk(w_in, o_g + i * D + n * 128, o_g + i * D + (n + 1) * 128))
            L.add(("br", i, n), _blk(br[i], n * 128, (n + 1) * 128))
    for n in range(16):
        L.add(("o", n), _blk(inp["w_o"], n * 128, (n + 1) * 128))
    up = inp["w_ffn_up"]
    dn = inp["w_ffn_down"]
    for kg in range(4):
        for jj in range(11):
            j = kg * 11 + jj
            L.add(("ug", j), _blk(up, j * 128, (j + 1) * 128))
            L.add(("uv", j), _blk(up, DFF + j * 128, DFF + (j + 1) * 128))
        for n in range(16):
            L.add(("dn", kg, n), _blk(dn[kg * 1408:(kg + 1) * 1408], n * 128, (n + 1) * 128))
    for n in range(8):
        L.add(("mkv", n), _blk(inp["w_mem_kv"], n * 128, (n + 1) * 128))
    wall = np.concatenate(L.parts, axis=1)
    L.parts = None
    return L, wall


CI = {}


def build_consts(inp):
    cols = []
    off = [0]

    def add(name, arr):
        arr = np.asarray(arr, np.float32)
        CI[name] = (off[0], arr.shape[1])
        cols.append(arr)
        off[0] += arr.shape[1]

    add("g_pre_mix", _pcol(inp["g_pre_mix"], 16))
    add("g_post_mix", _pcol(inp["g_post_mix"], 16))
    add("g_pre_ffn", _pcol(inp["g_pre_ffn"], 16))
    add("g_post_ffn", _pcol(inp["g_post_ffn"], 16))
    add("g_mem", _pcol(inp["g_mem"], 16))
    add("mu", _pcol(inp["mu_rwkv"], 20))
    for k in ("w0_decay", "a0", "k_k", "k_a", "ln_x_w", "ln_x_b"):
        add(k, _pcol(inp[k], 6))
    add("r_k", _pcol(inp["r_k"].reshape(-1), 6))
    add("sinks", _pcol(np.repeat(inp["swa_sinks"], 64), 6))
    add("conv_w", np.ascontiguousarray(inp["conv_w"].reshape(3, NFF, 128).transpose(2, 0, 1).reshape(128, 3 * NFF)))
    add("conv_b", _pcol(inp["conv_b"], NFF))
    add("wdau", np.concatenate([inp["w_decay_up"], inp["w_a_up"]], axis=0))
    add("wgu", inp["w_gate_up"])
    return np.ascontiguousarray(np.concatenate(cols, axis=1))


def build_program(L, ncols_wall, ncols_consts, dbg=None, stage=99, n_ptiles=4, do_sample=True):
    nc = bass.Bass("TRN2", target_bir_lowering=False)
    dt = nc.dram_tensor
    xp = dt("xp", [SEQ, D], F32, kind="ExternalInput").ap()
    xs = dt("xs", [128, D], F32, kind="ExternalInput").ap()
    memp = dt("memp", [256, D], F32, kind="ExternalInput").ap()
    csk = dt("csk", [16, 128, 256], F32, kind="ExternalInput").ap()
    csv = dt("csv", [16, 128, 256], F32, kind="ExternalInput").ap()
    cmk = dt("cmk", [16, 256, 512], F32, kind="ExternalInput").ap()
    cmv = dt("cmv", [16, 256, 512], F32, kind="ExternalInput").ap()
    sshift = dt("sshift", [16, RWP], F32, kind="ExternalInput").ap()
    swkv = dt("swkv", [16, 6, 128, 64], F32, kind="ExternalInput").ap()
    sconv = dt("sconv", [32, DFF], F32, kind="ExternalInput").ap()
    wall = dt("wall", [128, ncols_wall], F32, kind="ExternalInput").ap()
    consts_d = dt("consts", [128, ncols_consts], F32, kind="ExternalInput").ap()
    wscr = dt("wscr", [128, ncols_wall], BF16, kind="Internal").ap()

    yp = dt("yp", [SEQ // 2, D], F32, kind="ExternalOutput").ap()
    flag_d = dt("flag", [128, 1], F32, kind="ExternalInput").ap()
    ys = dt("ys", [128, D], F32, kind="ExternalOutput").ap()
    o_swk_p = dt("o_swk_p", [128, 256], F32, kind="ExternalOutput").ap()
    o_swv_p = dt("o_swv_p", [128, 256], F32, kind="ExternalOutput").ap()
    o_mk_p = dt("o_mk_p", [256, 512], F32, kind="ExternalOutput").ap()
    o_mv_p = dt("o_mv_p", [256, 512], F32, kind="ExternalOutput").ap()
    o_shift_p = dt("o_shift_p", [20, 128], F32, kind="ExternalOutput").ap()
    o_wkv_p = dt("o_wkv_p", [6, 128, 64], F32, kind="ExternalOutput").ap()
    o_conv_p = dt("o_conv_p", [2, DFF], F32, kind="ExternalOutput").ap()
    o_swk_s = dt("o_swk_s", [16, 128, 256], F32, kind="ExternalOutput").ap()
    o_swv_s = dt("o_swv_s", [16, 128, 256], F32, kind="ExternalOutput").ap()
    o_shift_s = dt("o_shift_s", [16, RWP], F32, kind="ExternalOutput").ap()
    o_wkv_s = dt("o_wkv_s", [16, 6, 128, 64], F32, kind="ExternalOutput").ap()
    o_conv_s = dt("o_conv_s", [32, DFF], F32, kind="ExternalOutput").ap()
    dbg_aps = {}
    if dbg:
        for k, shp in dbg.items():
            dbg_aps[k] = dt("dbg_" + k, list(shp), F32, kind="ExternalOutput").ap()

    es = contextlib.ExitStack()
    P = Prog(nc)

    def sb(name, shape, dtp=F32):
        return Buf(name, es.enter_context(nc.sbuf_tensor(name, list(shape), dtp)))

    def psb(name, shape):
        b_ = Buf(name, es.enter_context(nc.psum_tensor(name, list(shape), F32)))
        b_.psum = True
        return b_

    def mm(out, lhsT, rhs, start=True, stop=True):
        P.op("pe", lambda e: e.matmul(_ap(out), lhsT=_ap(lhsT), rhs=_ap(rhs), start=start, stop=stop,
                                      skip_group_check=True),
             _bufs(lhsT, rhs), _bufs(out))

    def tr(out, in_):
        P.op("pe", lambda e: e.transpose(_ap(out), _ap(in_), _ap(ident[0:in_.ap.shape[0], 0:in_.ap.shape[0]])),
             _bufs(in_, ident[:]), _bufs(out))

    def act(out, in_, func, bias=None, scale=None, eng="act"):
        kw = {}
        if bias is not None:
            kw["bias"] = _ap(bias)
        if scale is not None:
            kw["scale"] = _ap(scale)
        P.op(eng, lambda e: e.activation(out=_ap(out), in_=_ap(in_), func=func, **kw),
             _bufs(in_, bias, scale), _bufs(out))

    def tt(out, in0, in1, op, eng="dve"):
        P.op(eng, lambda e: e.tensor_tensor(out=_ap(out), in0=_ap(in0), in1=_ap(in1), op=op),
             _bufs(in0, in1), _bufs(out))

    def ts(out, in0, s1, s2, op0, op1=None, eng="dve"):
        if op1 is None:
            P.op(eng, lambda e: e.tensor_scalar(out=_ap(out), in0=_ap(in0), scalar1=_ap(s1), scalar2=None, op0=op0),
                 _bufs(in0, s1), _bufs(out))
        else:
            P.op(eng, lambda e: e.tensor_scalar(out=_ap(out), in0=_ap(in0), scalar1=_ap(s1), scalar2=_ap(s2),
                                                op0=op0, op1=op1),
                 _bufs(in0, s1, s2), _bufs(out))

    def stt(out, in0, scalar, in1, op0, op1):
        P.op("dve", lambda e: e.scalar_tensor_tensor(out=_ap(out), in0=_ap(in0), scalar=_ap(scalar), in1=_ap(in1),
                                                     op0=op0, op1=op1),
             _bufs(in0, scalar, in1), _bufs(out))

    def cp(out, in_, eng="dve"):
        if eng == "act":
            P.op("act", lambda e: e.copy(out=_ap(out), in_=_ap(in_)), _bufs(in_), _bufs(out))
        else:
            P.op(eng, lambda e: e.tensor_copy(out=_ap(out), in_=_ap(in_)), _bufs(in_), _bufs(out))

    def recip(out, in_):
        P.op("dve", lambda e: e.reciprocal(out=_ap(out), in_=_ap(in_)), _bufs(in_), _bufs(out))

    def memset(out, val, eng="pool"):
        P.op(eng, lambda e: e.memset(_ap(out), val), (), _bufs(out))

    def dma_in(out, in_ap, q="sp"):
        P.dma(q, lambda e: e.dma_start(out=_ap(out), in_=in_ap), (), _bufs(out))

    def dma_out(out_ap, in_, q="act"):
        P.dma(q, lambda e: e.dma_start(out=out_ap, in_=_ap(in_)), _bufs(in_), (), final=True)

    ident = sb("ident", [128, 128])
    ones = sb("ones", [128, 128])
    ones_r = sb("ones_r", [128, 128], BF16)
    blk1 = sb("blk1", [128, 128])
    cst = sb("cst", [128, ncols_consts])
    epsb = sb("epsb", [128, 1])
    omka = sb("omka", [128, 6])
    esink = sb("esink", [128, 6])
    for b_ in (ident, ones, ones_r, blk1, cst, epsb, omka, esink):
        b_.const = True

    def C(name, c0=None, c1=None):
        o, n = CI[name]
        if c0 is None:
            return cst[:, o:o + n]
        return cst[:, o + c0:o + (c1 if c1 is not None else c0 + 1)]

    dma_in(cst[:], consts_d[:, :])
    flagt = sb("flagt", [128, 1])
    flagt.const = True
    dma_in(flagt[:], flag_d[:, :])
    memset(ident[:], 0.0)
    P.op("pool", lambda e: e.affine_select(out=ident.t[:], in_=ident.t[:], pattern=[[-1, 128]],
                                           compare_op=ALU.not_equal, fill=1.0, base=0, channel_multiplier=1),
         [ident], [ident])
    memset(ones[:], 1.0)
    memset(ones_r[:], 1.0)
    memset(epsb[:], EPS)
    memset(blk1[:], 0.0)
    memset(blk1[0:64, 0:64], 1.0)
    memset(blk1[64:128, 64:128], 1.0)
    ts(omka[:], C("k_a"), -1.0, 1.0, ALU.mult, ALU.add)
    act(esink[:], C("sinks"), AF.Exp)

    def tri_mask(name, n, m, pattern_step, chan_mult, base, cmp):
        t = sb(name, [n, m])
        memset(t[:], 1.0)
        P.op("pool", lambda e: e.affine_select(out=t.t[:], in_=t.t[:], pattern=[[pattern_step, m]],
                                               compare_op=cmp, fill=0.0, base=base, channel_multiplier=chan_mult),
             [t], [t])
        t.const = True
        return t

    maskD = tri_mask("maskD", 128, 128, 1, -1, 0, ALU.is_ge)
    maskP = tri_mask("maskP", 128, 128, -1, 1, -1, ALU.is_ge)

    NPOOL = 42
    pool_free = [sb(f"cb{i}", [128, TT]) for i in range(NPOOL)]
    NRPOOL = 50
    rpool_free = [sb(f"rb{i}", [128, TT], BF16) for i in range(NRPOOL)]

    def ralloc(n=1):
        assert len(rpool_free) >= n, "bf16 pool exhausted"
        return [rpool_free.pop() for _ in range(n)]

    def rfree(bs):
        for b_ in bs:
            rpool_free.insert(0, b_)

    def alloc(n=1):
        assert len(pool_free) >= n, "chunk pool exhausted"
        r = [pool_free.pop() for _ in range(n)]
        return r

    def free(bs):
        for b_ in bs:
            pool_free.insert(0, b_)

    NSLOT = 4
    wslots = [sb(f"ws{i}", [128, 2048], BF16) for i in range(NSLOT)]
    wctr = [0]

    scr_ev = {}
    whs = [Buf(f"wh{i}", None) for i in range(NSLOT)]

    def wload(key):
        off, n = L.idx[key]
        s = wslots[wctr[0] % NSLOT]
        wh = whs[wctr[0] % NSLOT]
        wctr[0] += 1
        if key in scr_ev:
            P.dma("sp", lambda e: e.dma_start(out=s.t[:, 0:n], in_=wscr[:, off:off + n]), (), [s], extra=[scr_ev[key]],
                  sem_buf=wh)
        else:
            P.dma("pool", lambda e: e.dma_start(out=s.t[:, 0:n], in_=wall[:, off:off + n]), (), [s])
            if key[0] != "mkv" and USE_SCR:
                o = P.dma("sp", lambda e: e.dma_start(out=wscr[:, off:off + n], in_=s.t[:, 0:n]), [s], (), sem_buf=wh)
                scr_ev[key] = o.ev
        return s, n // 128

    banks = [psb(f"bank{i}", [128, 512]) for i in range(8)]
    gctr = [0]

    def gbank():
        b_ = banks[(0, 1, 6, 7)[gctr[0] % 4]]
        gctr[0] += 1
        return b_

    def gemm_fm(key, act_chunks, n_tok, evac, col0=0):
        s, kc = wload(key)
        assert kc == len(act_chunks), (key, kc, len(act_chunks))
        pb_ = gbank()
        for k in range(kc):
            mm(pb_[:, 0:n_tok], s[:, k * 128:(k + 1) * 128], act_chunks[k][:, col0:col0 + n_tok].r, start=(k == 0),
               stop=(k == kc - 1))
        evac(pb_[:, 0:n_tok])

    def dbg_out(name, view):
        if name in dbg_aps:
            dma_out(dbg_aps[name], view)

    def rms_rstd(chunks, n_tok):
        ssb = banks[3]
        nck = len(chunks)
        sq = ralloc(4)
        for c in range(nck):
            s_ = sq[c % 4]
            act(s_[:, 0:n_tok].r, chunks[c][:, 0:n_tok], AF.Square)
            mm(ssb[:, 0:n_tok], ones_r[:], s_[:, 0:n_tok].r, start=(c == 0), stop=(c == nck - 1))
        rfree(sq)
        rs = alloc(1)[0]
        act(rs[:, 0:n_tok], ssb[:, 0:n_tok], AF.Sqrt, bias=epsb[:], scale=1.0 / (nck * 128))
        recip(rs[:, 0:n_tok], rs[:, 0:n_tok])
        return rs

    gneps = sb("gneps", [128, 1])
    memset(gneps[:], 6.4e-4)
    gneps.const = True
    shiftI = sb("shiftI", [128, 128])
    memset(shiftI[:], 0.0)
    P.op("pool", lambda e: e.affine_select(out=shiftI.t[:], in_=shiftI.t[:], pattern=[[-1, 128]],
                                           compare_op=ALU.not_equal, fill=1.0, base=64, channel_multiplier=1),
         [shiftI], [shiftI])
    shiftI.const = True

    def build_mask5(name, CLv):
        t = sb(name, [CLv, 10 * CLv])
        memset(t[:], 1.0)
        for h2 in range(2):
            for b5 in range(5):
                o0 = (h2 * 5 + b5) * CLv
                if b5 in (0, 1):
                    base, cm, st = -1, -1, 1
                elif b5 in (3, 4):
                    base, cm, st = 0, -1, 1
                else:
                    base, cm, st = -1, 1, -1
                P.op("pool", lambda e, o0=o0, base=base, cm=cm, st=st: e.affine_select(
                    out=t.t[:, o0:o0 + CLv], in_=t.t[:, o0:o0 + CLv], pattern=[[st, CLv]],
                    compare_op=ALU.is_ge, fill=0.0, base=base, channel_multiplier=cm), [t], [t])
        t.const = True
        return t

    mask5p = build_mask5("mask5p", 64)
    mask5s = build_mask5("mask5s", 8)

    def build_rmask(name, CLv):
        t = sb(name, [128, TT])
        memset(t[:], 1.0)
        memset(t[:, :].rearrange("p (c t) -> p c t", t=CLv)[:, :, 0:1], 0.0)
        t.const = True
        return t

    rmaskp = build_rmask("rmaskp", 64)
    rmasks = build_rmask("rmasks", 8)


    Gb = [[sb(f"Gb{p_}{h_}", [64, 320]) for h_ in range(2)] for p_ in range(3)]
    TOKb = [sb(f"TOK{p_}", [64, 384]) for p_ in range(3)]
    TTb = [sb(f"TTb{p_}", [64, 128]) for p_ in range(3)]
    XYb = [[sb(f"XYb{q_}{l}", [64, 256], BF16) for l in range(2)] for q_ in range(2)]
    TThb = [sb(f"TTh{p_}", [64, 128], BF16) for p_ in range(3)]
    Wbb = [sb("Wb0", [64, 128]), sb("Wb1", [64, 128])]
    zpb = sb("zpb", [128, 8])
    cstg = sb("cstg", [128, 128])
    cout = sb("cout", [32, 256])
    stmp = sb("stmp", [128, 256])
    memset(stmp[:], 0.0)
    dummy = sb("dmy_a", [1, 4])
    dummy2 = sb("dmy_b", [1, 4])

    def rep4(name, src):
        t = sb(name, [128, 384])
        for k in range(3):
            cp(t[:, k * 128:(k + 1) * 128], src[:, :], eng="pool")
        t.const = True
        return t
    maskD4 = rep4("maskD4", maskD)
    maskP4 = rep4("maskP4", maskP)
    maskC96 = sb("maskC96", [128, 96])
    memset(maskC96[:], 1.0)
    P.op("pool", lambda e: e.affine_select(out=maskC96.t[:], in_=maskC96.t[:], pattern=[[0, 12], [-1, 8]],
                                           compare_op=ALU.is_ge, fill=0.0, base=-1, channel_multiplier=1),
         [maskC96], [maskC96])
    maskC96.const = True
    maskN = sb("maskN", [128, 128])
    memset(maskN[:], 1.0)
    for (pat, base, cm) in (([[-8, 16], [0, 8]], 0, 1), ([[8, 16], [0, 8]], 7, -1), ([[8, 16], [1, 8]], 0, -1)):
        P.op("pool", lambda e, pat=pat, base=base, cm=cm: e.affine_select(
            out=maskN.t[:], in_=maskN.t[:], pattern=pat, compare_op=ALU.is_ge, fill=0.0, base=base,
            channel_multiplier=cm), [maskN], [maskN])
    maskN4 = rep4("maskN4", maskN)
    Hz = sb("Hz", [128, 6, 2, 64])
    shc = sb("shc", [128, 20])
    czc = sb("czc", [128, NFF, 2])
    kdprev = sb("kdprev", [128, 4, 128])
    vprev = sb("vprev", [128, 256])
    memKT = sb("memKT", [128, 4, 256])
    memV = sb("memV", [128, 2, 512])
    memset(Hz[:], 0.0)
    memset(shc[:], 0.0)
    memset(czc[:], 0.0)
    memset(kdprev[:], 0.0)
    memset(vprev[:], 0.0)

    def mem_kv_stage():
        mt = alloc(8)
        for blk in range(2):
            for q in range(4):
                dma_in(mt[blk * 4 + q][:], memp[blk * 128:(blk + 1) * 128, q * 512:(q + 1) * 512])
        mT = alloc(16)
        for c in range(16):
            pb_ = banks[2]
            for blk in range(2):
                tr(pb_[:, blk * 128:(blk + 1) * 128], mt[blk * 4 + c // 4][:, (c % 4) * 128:(c % 4 + 1) * 128])
            cp(mT[c][:, 0:256], pb_[:, 0:256], eng="act")
        free(mt)
        rs = rms_rstd(mT, 256)
        mTn = ralloc(16)
        for c in range(16):
            stt(mTn[c][:, 0:256], mT[c][:, 0:256], C("g_mem", c), rs[:, 0:256], ALU.mult, ALU.mult)
        free([rs])
        free(mT)
        kvT = alloc(8)
        for n in range(8):
            def ev(ps, n=n):
                cp(kvT[n][:, 0:256], ps, eng="act")
            gemm_fm(("mkv", n), mTn, 256, ev)
        rfree(mTn)
        for h in range(4):
            cp(memKT[:, h, :].r, kvT[h][:, 0:256])
        kvtok = alloc(4)
        for blk in range(2):
            for half in range(2):
                pb_ = banks[2]
                for q in range(4):
                    n = half * 4 + q
                    tr(pb_[:, q * 128:(q + 1) * 128], kvT[n][:, blk * 128:(blk + 1) * 128])
                cp(kvtok[blk * 2 + half][:], pb_[:], eng="act")
            cp(memV[:, blk, :].r, kvtok[blk * 2 + 1][:])
            dma_out(o_mk_p[blk * 128:(blk + 1) * 128, :], kvtok[blk * 2][:])
            dma_out(o_mv_p[blk * 128:(blk + 1) * 128, :], kvtok[blk * 2 + 1][:])
        free(kvT)
        free(kvtok)

    def rwkv_stage(kind, ti, pT, orwb, NT, NSEQ, SL, CL, NCH, prompt, last_p, o_from=0):
        if not prompt:
            sst = alloc(5)
            for q_ in range(5):
                dma_in(sst[q_][0:16, :], sshift[:, q_ * 512:(q_ + 1) * 512])
            shT = alloc(1)[0]
            for c in range(20):
                pb_ = banks[2]
                tr(pb_[:, 0:16], sst[c // 4][0:16, (c % 4) * 128:(c % 4 + 1) * 128])
                cp(shT[:, c * 16:(c + 1) * 16], pb_[:, 0:16])
            free(sst)
            shout = alloc(5)
        dtmp = alloc(1)[0]
        for c in range(20):
            pv = pT[c][:, 0:NT].rearrange("p (b t) -> p b t", b=NSEQ)
            dv = dtmp[:, 0:NT].rearrange("p (b t) -> p b t", b=NSEQ)
            tt(dv[:, :, 1:SL], pv[:, :, 0:SL - 1], pv[:, :, 1:SL], ALU.subtract)
            if prompt:
                tt(dtmp[:, 0:1], shc[:, c:c + 1], pT[c][:, 0:1], ALU.subtract)
                cp(shc[:, c:c + 1], pT[c][:, NT - 1:NT], eng="dve")
            else:
                tt(dv[:, :, 0:1], shT[:, c * 16:(c + 1) * 16].rearrange("p (b t) -> p b t", t=1), pv[:, :, 0:1],
                   ALU.subtract)
                pb_ = banks[2]
                lastv = alloc(1)[0]
                cp(lastv[:, 0:16].rearrange("p (b t) -> p b t", t=1), pv[:, :, SL - 1:SL], eng="dve")
                tr(pb_[0:16, 0:128], lastv[:, 0:16])
                cp(shout[c // 4][0:16, (c % 4) * 128:(c % 4 + 1) * 128], pb_[0:16, 0:128], eng="act")
                free([lastv])
            stt(pT[c][:, 0:NT], dtmp[:, 0:NT], C("mu", c), pT[c][:, 0:NT], ALU.mult, ALU.add)
        free([dtmp])
        if not prompt:
            free([shT])
            for q_ in range(5):
                dma_out(o_shift_s[:, q_ * 512:(q_ + 1) * 512], shout[q_][0:16, :])
            free(shout)
        if last_p:
            pb_ = banks[2]
            tr(pb_[0:20, 0:128], shc[:, 0:20])
            so = alloc(1)[0]
            cp(so[0:20, 0:128], pb_[0:20, 0:128])
            dma_out(o_shift_p[:, :], so[0:20, 0:128])
            free([so])

        RWS = int(os.environ.get('RWS', 9)) if not prompt else 9
        if RWS < 2:
            return
        lr = pT[18]
        tanh_wl = alloc(1)[0]
        act(tanh_wl[0:64, 0:NT], lr[0:64, 0:NT], AF.Tanh)
        sig_gl = alloc(1)[0]
        act(sig_gl[:, 0:NT], pT[19][:, 0:NT], AF.Sigmoid)
        wdau = C("wdau")
        wgu = C("wgu")
        m5 = mask5p if prompt else mask5s
        rmask = rmaskp if prompt else rmasks
        NLEV = 6 if prompt else 3

        for j in range(6):
            r_, k_, v_ = pT[j], pT[6 + j], pT[12 + j]
            js = slice(j * 128, (j + 1) * 128)
            pb_ = banks[3]
            mm(pb_[:, 0:NT], wdau[0:64, js], tanh_wl[0:64, 0:NT])
            logw = alloc(1)[0]
            act(logw[:, 0:NT], pb_[:, 0:NT], AF.Sigmoid, bias=C("w0_decay", j))
            ts(logw[:, 0:NT], logw[:, 0:NT], -0.6065306597126334, None, ALU.mult, eng="dve")
            pb_ = banks[3]
            mm(pb_[:, 0:NT], wdau[64:128, js], lr[64:128, 0:NT])
            a_ = alloc(1)[0]
            act(a_[:, 0:NT], pb_[:, 0:NT], AF.Sigmoid, bias=C("a0", j))
            kk = alloc(1)[0]
            ts(kk[:, 0:NT], k_[:, 0:NT], C("k_k", j), None, ALU.mult)
            sq_ = alloc(1)[0]
            tt(sq_[:, 0:NT], kk[:, 0:NT], kk[:, 0:NT], ALU.mult, eng="dve")
            pb_ = banks[3]
            mm(pb_[:, 0:NT], blk1[:], sq_[:, 0:NT])
            act(sq_[:, 0:NT], pb_[:, 0:NT], AF.Sqrt)
            ts(sq_[:, 0:NT], sq_[:, 0:NT], 1e-12, None, ALU.max)
            recip(sq_[:, 0:NT], sq_[:, 0:NT])
            tt(kk[:, 0:NT], kk[:, 0:NT], sq_[:, 0:NT], ALU.mult)
            ts(sq_[:, 0:NT], a_[:, 0:NT], C("k_a", j), omka[:, j:j + 1], ALU.mult, ALU.add)
            tt(k_[:, 0:NT], k_[:, 0:NT], sq_[:, 0:NT], ALU.mult)
            stt(sq_[:, 0:NT], r_[:, 0:NT], C("r_k", j), k_[:, 0:NT], ALU.mult, ALU.mult)
            pb_ = banks[3]
            mm(pb_[:, 0:NT], blk1[:], sq_[:, 0:NT])
            bonus = alloc(1)[0]
            tt(bonus[:, 0:NT], pb_[:, 0:NT], v_[:, 0:NT], ALU.mult)
            cum = alloc(1)[0]
            P.op("dve", lambda e, cum=cum, logw=logw: e.tensor_tensor_scan(
                out=cum.t[:, 0:NT], data0=rmask.t[:, 0:NT], data1=logw.t[:, 0:NT], initial=0.0,
                op0=ALU.mult, op1=ALU.add), [rmask, logw], [cum])
            At, Rt, Bt, Kt, e1 = alloc(5)
            act(e1[:, 0:NT], cum[:, 0:NT], AF.Exp)
            tt(Rt[:, 0:NT], r_[:, 0:NT], e1[:, 0:NT], ALU.mult)
            tt(sq_[:, 0:NT], cum[:, 0:NT], logw[:, 0:NT], ALU.subtract, eng="dve")
            act(sq_[:, 0:NT], sq_[:, 0:NT], AF.Exp)
            stt(At[:, 0:NT], kk[:, 0:NT], -1.0, sq_[:, 0:NT], ALU.mult, ALU.mult)
            act(sq_[:, 0:NT], cum[:, 0:NT], AF.Exp, scale=-1.0)
            tt(Kt[:, 0:NT], k_[:, 0:NT], sq_[:, 0:NT], ALU.mult)
            tt(kk[:, 0:NT], kk[:, 0:NT], a_[:, 0:NT], ALU.mult, eng="dve")
            tt(Bt[:, 0:NT], kk[:, 0:NT], sq_[:, 0:NT], ALU.mult)
            free([kk, a_, logw, cum])
            Atb, Rtb, Btb, Ktb = ralloc(4)
            cp(Atb[:, 0:NT], At[:, 0:NT], eng="act")
            cp(Btb[:, 0:NT], Bt[:, 0:NT])
            cp(Ktb[:, 0:NT], Kt[:, 0:NT], eng="act")
            if o_from < NCH:
                cp(Rtb[:, 0:NT], Rt[:, 0:NT])

            if not prompt:
                s_in = alloc(2)
                for half in range(2):
                    dma_in(s_in[half][:, :].rearrange("p (b k) -> p b k", b=8),
                           swkv[half * 8:(half + 1) * 8, j].rearrange("b p k -> p b k"))
                s_out = alloc(2)

            ops_ = banks[4]

            def P1(c, par):
                cs = slice(c * CL, (c + 1) * CL)
                gpsb = [banks[5], banks[0]]
                for h2 in range(2):
                    hp = slice(h2 * 64, (h2 + 1) * 64)
                    gps = gpsb[h2]
                    ng = 5 if c >= o_from else 3
                    mm(gps[0:CL, 0 * CL:1 * CL], Btb[hp, cs], Atb[hp, cs])
                    mm(gps[0:CL, 1 * CL:2 * CL], Ktb[hp, cs], Atb[hp, cs])
                    mm(gps[0:CL, 2 * CL:3 * CL], Atb[hp, cs], Btb[hp, cs])
                    if c >= o_from:
                        mm(gps[0:CL, 3 * CL:4 * CL], Btb[hp, cs], Rtb[hp, cs])
                        mm(gps[0:CL, 4 * CL:5 * CL], Ktb[hp, cs], Rtb[hp, cs])
                    tt(Gb[par][h2][0:CL, 0:ng * CL], gps[0:CL, 0:ng * CL], m5[0:CL, 0:ng * CL], ALU.mult)
                tps = banks[6]
                tr(tps[0:CL, 0:128], v_[:, cs])
                tr(tps[0:CL, 128:256], Bt[:, cs])
                tr(tps[0:CL, 256:384], Kt[:, cs])
                cp(TOKb[par][0:CL, 0:384], tps[0:CL, 0:384], eng="act")
                yield
                GG = [Gb[par][h2][0:CL, 0:5 * CL] for h2 in range(2)]
                TTp = TTb[par]
                TTh = TThb[par]
                for h2 in range(2):
                    tt(TTp[0:CL, h2 * CL:(h2 + 1) * CL], GG[h2][:, 0:CL], ident[0:CL, 0:CL], ALU.add)
                cp(TTh[0:CL, 0:2 * CL], TTp[0:CL, 0:2 * CL], eng="act")
                XYs = XYb[c % 2]
                sqbank = banks[7] if c % 2 == 0 else banks[3]
                xy = XYs[0]
                for h2 in range(2):
                    cp(xy[0:CL, (h2 * 2) * CL:(h2 * 2 + 1) * CL], GG[h2][:, 0:CL], eng="act")
                    cp(xy[0:CL, (h2 * 2 + 1) * CL:(h2 * 2 + 2) * CL], GG[h2][:, 2 * CL:3 * CL])
                X = [xy[0:CL, (h2 * 2) * CL:(h2 * 2 + 1) * CL] for h2 in range(2)]
                Y = [xy[0:CL, (h2 * 2 + 1) * CL:(h2 * 2 + 2) * CL] for h2 in range(2)]
                for s_ in range(1, NLEV + 1):
                    sps = sqbank
                    do_sq = s_ <= NLEV - 1
                    do_tu = s_ >= 2
                    if do_sq:
                        for h2 in range(2):
                            mm(sps[0:CL, (h2 * 2) * CL:(h2 * 2 + 1) * CL], Y[h2], X[h2])
                            mm(sps[0:CL, (h2 * 2 + 1) * CL:(h2 * 2 + 2) * CL], X[h2], Y[h2])
                    if do_tu:
                        for h2 in range(2):
                            mm(sps[0:CL, (4 + h2) * CL:(5 + h2) * CL], Y[h2], TTh[0:CL, h2 * CL:(h2 + 1) * CL])
                    if do_sq:
                        xy = XYs[s_ % 2]
                        cp(xy[0:CL, 0:4 * CL], sps[0:CL, 0:4 * CL], eng="act")
                    if do_tu:
                        tt(TTp[0:CL, 0:2 * CL], TTp[0:CL, 0:2 * CL], sps[0:CL, 4 * CL:6 * CL], ALU.add)
                        if s_ < NLEV:
                            cp(TTh[0:CL, 0:2 * CL], TTp[0:CL, 0:2 * CL])
                    if do_sq:
                        X = [xy[0:CL, (h2 * 2) * CL:(h2 * 2 + 1) * CL] for h2 in range(2)]
                        Y = [xy[0:CL, (h2 * 2 + 1) * CL:(h2 * 2 + 2) * CL] for h2 in range(2)]
                    yield

            def P2(c, par):
                cs = slice(c * CL, (c + 1) * CL)
                GG = [Gb[par][h2][0:CL, 0:5 * CL] for h2 in range(2)]
                TOK = TOKb[par]
                TTp = TTb[par]
                Vt = [TOK[0:CL, h2 * 64:(h2 + 1) * 64] for h2 in range(2)]
                Btk = [TOK[0:CL, 128 + h2 * 64:128 + (h2 + 1) * 64] for h2 in range(2)]
                Ktk = [TOK[0:CL, 256 + h2 * 64:256 + (h2 + 1) * 64] for h2 in range(2)]
                if not prompt:
                    pb_ = banks[2]
                    sv = s_in[c // 8][:, (c % 8) * 64:(c % 8 + 1) * 64]
                    tr(pb_[0:64, 0:128], sv)
                    cp(Hz[0:64, j, 0, :], pb_[0:64, 0:64])
                    cp(stmp[0:64, 0:64], pb_[0:64, 64:128], eng="act")
                    mm(pb_[:, 128:192], shiftI[:, :], stmp[:, 0:64])
                    cp(Hz[64:128, j, 1, :], pb_[64:128, 128:192])
                wb = banks[1]
                for h2 in range(2):
                    mm(wb[0:CL, h2 * 64:(h2 + 1) * 64], At[:, cs], Hz[:, j, h2, :], start=True, stop=False)
                    mm(wb[0:CL, h2 * 64:(h2 + 1) * 64], GG[h2][:, 1 * CL:2 * CL], Vt[h2], start=False, stop=True)
                cp(Wbb[0][0:CL, 0:128], wb[0:CL, 0:128])
                yield
                for h2 in range(2):
                    mm(wb[0:CL, 128 + h2 * 64:128 + (h2 + 1) * 64], TTp[0:CL, h2 * CL:(h2 + 1) * CL],
                       Wbb[0][0:CL, h2 * 64:(h2 + 1) * 64])
                cp(Wbb[1][0:CL, 0:128], wb[0:CL, 128:256], eng="act")
                U = [Wbb[1][0:CL, h2 * 64:(h2 + 1) * 64] for h2 in range(2)]
                yield
                for h2 in (range(2) if c >= o_from else []):
                    hp = slice(h2 * 64, (h2 + 1) * 64)
                    mm(ops_[hp, cs], Hz[:, j, h2, :], Rt[:, cs], start=True, stop=False)
                    mm(ops_[hp, cs], U[h2], GG[h2][:, 3 * CL:4 * CL], start=False, stop=False)
                    mm(ops_[hp, cs], Vt[h2], GG[h2][:, 4 * CL:5 * CL], start=False, stop=True)
                for h2 in range(2):
                    hp = slice(h2 * 64, (h2 + 1) * 64)
                    if not HADD:
                        mm(wb[hp, 256:320], ident[:, hp], Hz[:, j, h2, :], start=True, stop=False)
                    mm(wb[hp, 256:320], Btk[h2], U[h2], start=bool(HADD), stop=False)
                    mm(wb[hp, 256:320], Ktk[h2], Vt[h2], start=False, stop=True)
                gcol = slice((c + 1) * CL - 1, (c + 1) * CL)
                if HADD:
                    tt(Hz[0:64, j, 0, :], wb[0:64, 256:320], Hz[0:64, j, 0, :], ALU.add)
                    tt(Hz[64:128, j, 1, :], wb[64:128, 256:320], Hz[64:128, j, 1, :], ALU.add)
                    act(Hz[0:64, j, 0, :], Hz[0:64, j, 0, :], AF.Copy, scale=e1[0:64, gcol])
                    act(Hz[64:128, j, 1, :], Hz[64:128, j, 1, :], AF.Copy, scale=e1[64:128, gcol])
                else:
                    act(Hz[0:64, j, 0, :], wb[0:64, 256:320], AF.Copy, scale=e1[0:64, gcol])
                    act(Hz[64:128, j, 1, :], wb[64:128, 256:320], AF.Copy, scale=e1[64:128, gcol])
                if not prompt:
                    pb_ = banks[2]
                    mm(pb_[0:64, 256:320], Hz[:, j, 0, :], ident[:, 0:64])
                    mm(pb_[0:64, 320:384], Hz[:, j, 1, :], ident[:, 64:128])
                    so_ = s_out[c // 8]
                    cc = slice((c % 8) * 64, (c % 8 + 1) * 64)
                    cp(so_[0:64, cc], pb_[0:64, 256:320])
                    cp(stmp[0:64, 128:192], pb_[0:64, 320:384], eng="act")
                    mm(pb_[:, 192:256], shiftI[:, :], stmp[:, 128:192])
                    cp(so_[64:128, cc], pb_[64:128, 192:256])
                yield

            def drive(gens):
                gens = list(gens)
                while gens:
                    for g_ in list(gens):
                        try:
                            next(g_)
                        except StopIteration:
                            gens.remove(g_)

            active = []
            p1_done = set()
            nxt1, nxt2, p2_done = 0, 0, -1
            while nxt2 < NCH or active:
                while (nxt1 < NCH and sum(1 for a in active if a[0] == 1) < 2 and nxt1 <= p2_done + 3):
                    active.append((1, nxt1, P1(nxt1, nxt1 % 3)))
                    nxt1 += 1
                if nxt2 < NCH and not any(a[0] == 2 for a in active) and nxt2 in p1_done:
                    active.append((2, nxt2, P2(nxt2, nxt2 % 3)))
                    nxt2 += 1
                for a in list(active):
                    try:
                        next(a[2])
                    except StopIteration:
                        active.remove(a)
                        if a[0] == 1:
                            p1_done.add(a[1])
                        else:
                            p2_done = a[1]
            if not prompt:
                for half in range(2):
                    dma_out(o_wkv_s[half * 8:(half + 1) * 8, j].rearrange("b p k -> p b k"),
                            s_out[half][:, :].rearrange("p (b k) -> p b k", b=8))
                free(s_in + s_out)
            free([e1, At, Rt, Bt, Kt])
            rfree([Atb, Rtb, Btb, Ktb])
            PC0 = o_from * CL
            if o_from < NCH:
                orw = pT[j]
                cp(orw[:, PC0:NT], ops_[:, PC0:NT], eng="act")
                pb_ = banks[3]
                mm(pb_[:, PC0:NT], blk1[:], orw[:, PC0:NT])
                stt(orw[:, PC0:NT], pb_[:, PC0:NT], -1.0 / 64, orw[:, PC0:NT], ALU.mult, ALU.add)
                tt(sq_[:, PC0:NT], orw[:, PC0:NT], orw[:, PC0:NT], ALU.mult, eng="dve")
                pb_ = banks[3]
                mm(pb_[:, PC0:NT], blk1[:], sq_[:, PC0:NT])
                act(sq_[:, PC0:NT], pb_[:, PC0:NT], AF.Sqrt, bias=gneps[:], scale=1.0 / 64)
                recip(sq_[:, PC0:NT], sq_[:, PC0:NT])
                tt(orw[:, PC0:NT], orw[:, PC0:NT], sq_[:, PC0:NT], ALU.mult)
                ts(orw[:, PC0:NT], orw[:, PC0:NT], C("ln_x_w", j), C("ln_x_b", j), ALU.mult, ALU.add)
                tt(orw[:, PC0:NT], orw[:, PC0:NT], bonus[:, PC0:NT], ALU.add)
                pb_ = banks[3]
                mm(pb_[:, PC0:NT], wgu[:, js], sig_gl[:, PC0:NT])
                tt(orwb[j][:, PC0:NT], orw[:, PC0:NT], pb_[:, PC0:NT], ALU.mult)
            free([sq_, bonus])
        free([tanh_wl, sig_gl])
        if last_p:
            so = alloc(1)[0]
            for j in range(6):
                pb_ = banks[2]
                mm(pb_[0:64, 0:64], Hz[:, j, 0, :], ident[:, 0:64])
                mm(pb_[0:64, 64:128], Hz[:, j, 1, :], ident[:, 64:128])
                cp(so[0:64, j * 64:(j + 1) * 64], pb_[0:64, 0:64])
                cp(stmp[0:64, 0:64], pb_[0:64, 64:128], eng="act")
                mm(pb_[:, 128:192], shiftI[:, :], stmp[:, 0:64])
                cp(so[64:128, j * 64:(j + 1) * 64], pb_[64:128, 128:192])
            dma_out(o_wkv_p.rearrange("j p k -> p j k"), so[:, 0:384].rearrange("p (j k) -> p j k", j=6))
            free([so])

    def load_xT(xsrc, NBK):
        xT = alloc(16)
        for blk in range(NBK):
            xt = alloc(4)
            for q in range(4):
                dma_in(xt[q][:], xsrc[blk * 128:(blk + 1) * 128, q * 512:(q + 1) * 512])
            for c in range(16):
                pb_ = banks[2 + (c % 2)]
                tr(pb_[:, 0:128], xt[c // 4][:, (c % 4) * 128:(c % 4 + 1) * 128])
                cp(xT[c][:, blk * 128:(blk + 1) * 128], pb_[:, 0:128], eng=("act" if c % 2 else "dve"))
            free(xt)
        return xT

    def tile_stage(kind, ti, mode="full"):
        prompt = kind == "p"
        NT = TT if prompt else 128
        NBK = NT // 128
        CL = 64 if prompt else 8
        NCH = NT // CL
        NSEQ = 1 if prompt else 16
        SL = NT // NSEQ
        xsrc = xp[ti * TT:(ti + 1) * TT, :] if prompt else xs
        last_p = prompt and ti == n_ptiles - 1
        d0 = (ti == 0)

        xT = load_xT(xsrc, NBK)
        rs = rms_rstd(xT, NT)
        xn = ralloc(16)
        for c in range(16):
            stt(xn[c][:, 0:NT], xT[c][:, 0:NT], C("g_pre_mix", c), rs[:, 0:NT], ALU.mult, ALU.mult)
        free([rs])
        free(xT)

        if mode == "rwA":
            orwb = ralloc(6)
            pT = alloc(20)
            for n in range(20):
                gemm_fm(("rw", n), xn, NT, lambda ps, n=n: cp(pT[n][:, 0:NT], ps, eng="act"))
            rwkv_stage(kind, ti, pT, orwb, NT, NSEQ, SL, CL, NCH, prompt, last_p, o_from=NCH)
            free(pT)
            rfree(orwb)
            rfree(xn)
            return

        q = alloc(6)
        kd = alloc(4)
        vt = alloc(2)
        qm = alloc(4)
        osw = ralloc(6)
        omem = ralloc(4)
        orwb = ralloc(6)
        for n in range(6):
            gemm_fm(("q", n), xn, NT, lambda ps, n=n: cp(q[n][:, 0:NT], ps, eng="act"))
        for h in range(4):
            gemm_fm(("kd", h), xn, NT, lambda ps, h=h: cp(kd[h][:, 0:NT], ps, eng="act"))
        for n in range(2):
            s, kc = wload(("v", n))
            pb_ = gbank()
            for blk in range(NBK):
                for k in range(16):
                    mm(pb_[:, blk * 128:(blk + 1) * 128], xn[k][:, blk * 128:(blk + 1) * 128].r,
                       s[:, k * 128:(k + 1) * 128], start=(k == 0), stop=(k == 15))
            cp(vt[n][:, 0:NT], pb_[:, 0:NT], eng="act")
        for n in range(4):
            gemm_fm(("qm", n), xn, NT, lambda ps, n=n: cp(qm[n][:, 0:NT].r, ps, eng="act"))
        if d0:
            dbg_out("q" + kind, q[1][:, 0:NT])
            dbg_out("vt" + kind, vt[1][:, 0:NT])

        def Vh(blk, hk):
            return vt[hk // 2][:, blk * 128 + (hk % 2) * 64: blk * 128 + (hk % 2) * 64 + 64]

        if stage >= 2:
            if prompt:
                for qb in (range(NBK) if mode != "mixB" else [NBK - 1]):
                    qs = slice(qb * 128, (qb + 1) * 128)
                    gblk = ti * NBK + qb
                    kbl = []
                    if gblk > 0:
                        if qb == 0:
                            kbl.append(("P", lambda hk, par: kdprev[par * 64:(par + 1) * 64, hk, :],
                                        lambda hk: vprev[:, hk * 64:(hk + 1) * 64]))
                        else:
                            kbl.append(("P", lambda hk, par, qb=qb: kd[hk][par * 64:(par + 1) * 64, (qb - 1) * 128:qb * 128],
                                        lambda hk, qb=qb: Vh(qb - 1, hk)))
                    kbl.append(("D", lambda hk, par, qb=qb: kd[hk][par * 64:(par + 1) * 64, qb * 128:(qb + 1) * 128],
                                lambda hk, qb=qb: Vh(qb, hk)))
                    nkb = len(kbl)
                    for cset in range(2):
                        ob, db = banks[2], banks[3]
                        for par in range(2):
                            heads = [2 * (3 * cset + k) + par for k in range(3)]
                            Es = alloc(nkb)
                            for kbi, (typ, kf, vf) in enumerate(kbl):
                                sbk = banks[4 + 2 * kbi + par]
                                for k, i in enumerate(heads):
                                    mm(sbk[:, k * 128:(k + 1) * 128], kf(i // 3, par),
                                       q[i // 2][par * 64:(par + 1) * 64, qs])
                                act(Es[kbi][:, 0:384], sbk[:, 0:384], AF.Exp, scale=0.125)
                                tt(Es[kbi][:, 0:384], Es[kbi][:, 0:384], (maskP4 if typ == "P" else maskD4)[:, 0:384],
                                   ALU.mult, eng="dve")
                                if typ == "P" and qb == 0 and ti == 2:
                                    ts(Es[kbi][:, 0:384], Es[kbi][:, 0:384], flagt[:, 0:1], None, ALU.mult)
                            for k, i in enumerate(heads):
                                rp, rc = slice(par * 64, (par + 1) * 64), slice(k * 128, (k + 1) * 128)
                                for kbi, (typ, kf, vf) in enumerate(kbl):
                                    mm(ob[rp, rc], vf(i // 3), Es[kbi][:, k * 128:(k + 1) * 128],
                                       start=(kbi == 0), stop=(kbi == nkb - 1))
                                for kbi, (typ, kf, vf) in enumerate(kbl):
                                    mm(db[rp, rc], ones[:, 0:64], Es[kbi][:, k * 128:(k + 1) * 128],
                                       start=(kbi == 0), stop=(kbi == nkb - 1))
                            free(Es)
                        tmp = alloc(1)[0]
                        for k in range(3):
                            ci = 3 * cset + k
                            act(tmp[:, k * 128:(k + 1) * 128], db[:, k * 128:(k + 1) * 128], AF.Identity,
                                bias=esink[:, ci:ci + 1])
                        recip(tmp[:, 0:384], tmp[:, 0:384])
                        for k in range(3):
                            ci = 3 * cset + k
                            tt(osw[ci][:, qs], ob[:, k * 128:(k + 1) * 128], tmp[:, k * 128:(k + 1) * 128], ALU.mult)
                        free([tmp])
                for hk in range(4):
                    cp(kdprev[:, hk, :], kd[hk][:, NT - 128:NT], eng="dve")
                for n in range(2):
                    cp(vprev[:, n * 128:(n + 1) * 128], vt[n][:, NT - 128:NT], eng="dve")
                if last_p:
                    pb_ = banks[2]
                    for hk in range(4):
                        tr(pb_[:, hk * 64:(hk + 1) * 64], kd[hk][0:64, NT - 128:NT])
                    kt = alloc(1)[0]
                    cp(kt[:, 0:256], pb_[:, 0:256])
                    dma_out(o_swk_p[:, :], kt[:, 0:256])
                    dma_out(o_swv_p[:, :], vprev[:, :])
                    free([kt])
            else:
                P.dma("sp", lambda e: e.dma_start(out=o_swk_s[:, 0:120, :], in_=csk[:, 8:128, :]), (), [dummy], final=True)
                P.dma("sp", lambda e: e.dma_start(out=o_swv_s[:, 0:120, :], in_=csv[:, 8:128, :]), (), [dummy2], final=True)
                pb_ = banks[2]
                for hk in range(4):
                    tr(pb_[:, hk * 64:(hk + 1) * 64], kd[hk][0:64, 0:128])
                kt = alloc(1)[0]
                cp(kt[:, 0:256], pb_[:, 0:256])
                cp(kt[:, 256:384], vt[0][:, 0:128])
                cp(kt[:, 384:512], vt[1][:, 0:128])
                for b in range(16):
                    dma_out(o_swk_s[b, 120:128, :], kt[b * 8:(b + 1) * 8, 0:256])
                    dma_out(o_swv_s[b, 120:128, :], kt[b * 8:(b + 1) * 8, 256:512])
                free([kt])
                OC = [banks[5], banks[6]]
                DEN = [banks[0], banks[1]]
                SB2 = [banks[4], banks[3]]
                kcd = alloc(2)
                vcb = alloc(2)
                kcT = alloc(2)
                Eb = alloc(2)
                for b in range(16):
                    kc_ = kcd[b % 2]
                    vc_ = vcb[b % 2]
                    kv = kc_[:, :].rearrange("p (h u d) -> p h u d", h=4, u=2)
                    src = csk[b].rearrange("p (h d) -> p h d", h=4)
                    dma_in(kv[:, :, 0, :], src)
                    dma_in(kv[:, :, 1, :], src)
                    dma_in(vc_[:, 0:256], csv[b])
                    pb_ = banks[2]
                    for hk in range(4):
                        tr(pb_[:, hk * 128:(hk + 1) * 128], kc_[:, hk * 128:(hk + 1) * 128])
                    kT = kcT[b % 2]
                    cp(kT[:], pb_[:], eng="act")
                    for par in range(2):
                        for k in range(6):
                            i = 2 * k + par
                            hk = i // 3
                            mm(SB2[par][:, k * 8:(k + 1) * 8], kT[par * 64:(par + 1) * 64, hk * 128:(hk + 1) * 128],
                               q[k][par * 64:(par + 1) * 64, b * 8:(b + 1) * 8])
                    E_ = Eb[b % 2]
                    for par in range(2):
                        act(E_[:, par * 48:(par + 1) * 48], SB2[par][:, 0:48], AF.Exp, scale=0.125)
                    tt(E_[:, 0:96], E_[:, 0:96], maskC96[:, :], ALU.mult)
                    for i in range(12):
                        par = i % 2
                        hk = i // 3
                        ci = i // 2
                        col = (ci % 4) * 128 + b * 8
                        ec = par * 48 + ci * 8
                        mm(OC[ci // 4][par * 64:(par + 1) * 64, col:col + 8], vc_[:, hk * 64:(hk + 1) * 64],
                           E_[:, ec:ec + 8])
                        mm(DEN[ci // 4][par * 64:(par + 1) * 64, col:col + 8], ones[:, 0:64], E_[:, ec:ec + 8])
                free(kcd + vcb + kcT + Eb)
                for cset in range(2):
                    onb, dnb = banks[7], banks[2]
                    for par in range(2):
                        heads = [2 * (3 * cset + k) + par for k in range(3)]
                        sbk = SB2[par]
                        for k, i in enumerate(heads):
                            mm(sbk[:, k * 128:(k + 1) * 128], kd[i // 3][par * 64:(par + 1) * 64, 0:128],
                               q[i // 2][par * 64:(par + 1) * 64, 0:128])
                        En = alloc(1)[0]
                        act(En[:, 0:384], sbk[:, 0:384], AF.Exp, scale=0.125)
                        tt(En[:, 0:384], En[:, 0:384], maskN4[:, 0:384], ALU.mult)
                        for k, i in enumerate(heads):
                            rp, rc = slice(par * 64, (par + 1) * 64), slice(k * 128, (k + 1) * 128)
                            mm(onb[rp, rc], Vh(0, i // 3), En[:, k * 128:(k + 1) * 128])
                            mm(dnb[rp, rc], ones[:, 0:64], En[:, k * 128:(k + 1) * 128])
                        free([En])
                    ons, dns, tmp, tmp2 = alloc(4)
                    cp(ons[:, 0:384], onb[:, 0:384], eng="act")
                    cp(dns[:, 0:384], dnb[:, 0:384], eng="act")
                    for k in range(3):
                        ci = 3 * cset + k
                        csl = slice((ci % 4) * 128, (ci % 4 + 1) * 128)
                        ks = slice(k * 128, (k + 1) * 128)
                        tt(tmp[:, ks], DEN[ci // 4][:, csl], dns[:, ks], ALU.add)
                        act(tmp[:, ks], tmp[:, ks], AF.Identity, bias=esink[:, ci:ci + 1])
                        tt(tmp2[:, ks], OC[ci // 4][:, csl], ons[:, ks], ALU.add)
                    recip(tmp[:, 0:384], tmp[:, 0:384])
                    for k in range(3):
                        ci = 3 * cset + k
                        ks = slice(k * 128, (k + 1) * 128)
                        tt(osw[ci][:, 0:128], tmp2[:, ks], tmp[:, ks], ALU.mult)
                    free([ons, dns, tmp, tmp2])
            if prompt:
                for h in range(4):
                    Es = alloc(2)
                    for mb in range(2):
                        sbk = banks[4 + mb]
                        mm(sbk[:, 0:NT], memKT[:, h, mb * 128:(mb + 1) * 128].r, qm[h][:, 0:NT].r)
                        act(Es[mb][:, 0:NT].r, sbk[:, 0:NT], AF.Exp, scale=128 ** -0.5)
                    ob, db = banks[6], banks[7]
                    for mb in range(2):
                        mm(ob[:, 0:NT], memV[:, mb, h * 128:(h + 1) * 128].r, Es[mb][:, 0:NT].r, start=(mb == 0), stop=(mb == 1))
                    for mb in range(2):
                        mm(db[:, 0:NT], ones[:], Es[mb][:, 0:NT], start=(mb == 0), stop=(mb == 1))
                    free(Es)
                    tmp = alloc(1)[0]
                    recip(tmp[:, 0:NT], db[:, 0:NT])
                    tt(omem[h][:, 0:NT], ob[:, 0:NT], tmp[:, 0:NT], ALU.mult)
                    free([tmp])
            else:
                OM, DM = banks[6], banks[7]
                kmb = alloc(4)
                vmb = alloc(4)
                kmT = alloc(4)
                Eb = alloc(2)
                for b in range(16):
                    pp = (b % 2) * 2
                    for mb in range(2):
                        dma_in(kmb[pp + mb][:], cmk[b, mb * 128:(mb + 1) * 128, :])
                        dma_in(vmb[pp + mb][:], cmv[b, mb * 128:(mb + 1) * 128, :])
                        pb_ = banks[3 if mb else 2]
                        for h in range(4):
                            tr(pb_[:, h * 128:(h + 1) * 128], kmb[pp + mb][:, h * 128:(h + 1) * 128])
                        cp(kmT[pp + mb][:], pb_[:], eng=("act" if mb else "dve"))
                    sbk = banks[4]
                    for mb in range(2):
                        for h in range(4):
                            mm(sbk[:, (mb * 4 + h) * 8:(mb * 4 + h + 1) * 8], kmT[pp + mb][:, h * 128:(h + 1) * 128],
                               qm[h][:, b * 8:(b + 1) * 8])
                    E_ = Eb[b % 2]
                    act(E_[:, 0:64], sbk[:, 0:64], AF.Exp, scale=128 ** -0.5)
                    for h in range(4):
                        for mb in range(2):
                            mm(OM[:, h * 128 + b * 8:h * 128 + (b + 1) * 8], vmb[pp + mb][:, h * 128:(h + 1) * 128],
                               E_[:, (mb * 4 + h) * 8:(mb * 4 + h + 1) * 8], start=(mb == 0), stop=(mb == 1))
                        for mb in range(2):
                            mm(DM[:, h * 128 + b * 8:h * 128 + (b + 1) * 8], ones[:],
                               E_[:, (mb * 4 + h) * 8:(mb * 4 + h + 1) * 8], start=(mb == 0), stop=(mb == 1))
                free(kmb + vmb + kmT + Eb)
                tmp = alloc(1)[0]
                recip(tmp[:], DM[:])
                for h in range(4):
                    tt(omem[h][:, 0:128], OM[:, h * 128:(h + 1) * 128], tmp[:, h * 128:(h + 1) * 128], ALU.mult)
                free([tmp])
        free(kd + vt)
        free(q + qm)

        pT = alloc(20)
        for n in range(20):
            gemm_fm(("rw", n), xn, NT, lambda ps, n=n: cp(pT[n][:, 0:NT], ps, eng="act"))
        if d0:
            dbg_out("prw" + kind, pT[7][:, 0:NT])
        if stage >= 3:
            rwkv_stage(kind, ti, pT, orwb, NT, NSEQ, SL, CL, NCH, prompt, last_p,
                       o_from=(NCH - 2 if mode == "mixB" else 0))
        free(pT)

        NE = 128 if mode == "mixB" else NT
        CE = NT - NE
        merged = ralloc(16)
        obr = [orwb, osw, omem]
        for n in range(16):
            gsb = alloc(3)
            for i in range(3):
                gemm_fm(("g", i, n), xn, NE, lambda ps, i=i: act(gsb[i][:, 0:NE], ps, AF.Sigmoid), col0=CE)
                if i == 0:
                    gemm_fm(("br", i, n), obr[i], NE,
                            lambda ps, n=n: tt(gsb[0][:, 0:NE], ps, gsb[0][:, 0:NE], ALU.mult), col0=CE)
                else:
                    def ev(ps, i=i, n=n):
                        tt(gsb[i][:, 0:NE], ps, gsb[i][:, 0:NE], ALU.mult)
                        if i == 1:
                            tt(gsb[0][:, 0:NE], gsb[0][:, 0:NE], gsb[1][:, 0:NE], ALU.add)
                        else:
                            tt(merged[n][:, 0:NE], gsb[0][:, 0:NE], gsb[2][:, 0:NE], ALU.add)
                    gemm_fm(("br", i, n), obr[i], NE, ev, col0=CE)
            free(gsb)
        rfree(xn)
        rfree(orwb + osw + omem)

        post = alloc(16)
        for n in range(16):
            gemm_fm(("o", n), merged, NE, lambda ps, n=n: cp(post[n][:, 0:NE], ps, eng="act"))
        rfree(merged)
        rs = rms_rstd(post, NE)
        hT = load_xT(xsrc[CE:NT, :], NE // 128)
        for n in range(16):
            stt(post[n][:, 0:NE], post[n][:, 0:NE], C("g_post_mix", n), rs[:, 0:NE], ALU.mult, ALU.mult)
            tt(hT[n][:, 0:NE], hT[n][:, 0:NE], post[n][:, 0:NE], ALU.add, eng="dve")
        free([rs])
        free(post)
        if d0:
            dbg_out("h" + kind, hT[5][:, 0:NE])
        rs = rms_rstd(hT, NE)
        hn = ralloc(16)
        for n in range(16):
            stt(hn[n][:, 0:NE], hT[n][:, 0:NE], C("g_pre_ffn", n), rs[:, 0:NE], ALU.mult, ALU.mult)
        free([rs])

        if mode == "mixB":
            for j in range(NFF):
                gemm_fm(("ug", j), hn, NE, lambda ps, j=j: cp(czc[:, j, :], ps[:, NE - 2:NE], eng="act"))
            ts(czc[:, :, :].rearrange("p j t -> p (j t)"), czc[:, :, :].rearrange("p j t -> p (j t)"), flagt[:, 0:1], None,
               ALU.mult)
            rfree(hn)
            free(hT)
            return

        yacc = alloc(16)
        cw, cb_ = CI["conv_w"][0], CI["conv_b"][0]
        NG, GS = 4, 11
        W2 = SL + 2
        fbuf = [ralloc(GS), ralloc(GS)]
        GCH = {}
        stT_cache = {}

        def sample_state(grp4):
            st_in = alloc(1)[0]
            dma_in(st_in[0:32, :], sconv[:, grp4 * 512:(grp4 + 1) * 512])
            pb_ = banks[2]
            for jj in range(4):
                tr(pb_[:, jj * 32:(jj + 1) * 32], st_in[0:32, jj * 128:(jj + 1) * 128])
            stT = alloc(1)[0]
            cp(stT[:, 0:128], pb_[:, 0:128])
            free([st_in])
            return stT

        GST = {}

        def GATE_MM(j):
            w2c = cst[:, cw + 2 * NFF + j:cw + 2 * NFF + j + 1]
            bc = cst[:, cb_ + j:cb_ + j + 1]
            zc, c_, u_ = alloc(3)
            GST[j] = (zc, c_, u_)
            if prompt:
                def ev(ps):
                    cp(zc[:, 0:NT], ps, eng="act")
                    act(c_[:, 0:NT], ps, AF.Identity, bias=bc, scale=w2c)
            else:
                zall = zc[:, 0:NSEQ * W2].rearrange("p (b t) -> p b t", b=NSEQ)

                def ev(ps):
                    cp(zall[:, :, 2:W2], ps.rearrange("p (b t) -> p b t", b=NSEQ), eng="act")
                    act(c_[:, 0:NT], ps, AF.Identity, bias=bc, scale=w2c)
            gemm_fm(("ug", j), hn, NT, ev)

        def CONV_OUT(jo):
            if jo < 0:
                return
            pb_ = banks[2]
            if prompt:
                tr(pb_[0:2, 0:128], czc[:, jo, :])
                cp(cout[0:2, (jo % 2) * 128:(jo % 2 + 1) * 128], pb_[0:2, 0:128], eng="act")
                dma_out(o_conv_p[:, jo * 128:(jo + 1) * 128], cout[0:2, (jo % 2) * 128:(jo % 2 + 1) * 128])
            else:
                tr(pb_[0:32, 128:256], cstg[:, (jo % 4) * 32:(jo % 4 + 1) * 32])
                cp(cout[0:32, (jo % 2) * 128:(jo % 2 + 1) * 128], pb_[0:32, 128:256], eng="act")
                dma_out(o_conv_s[:, jo * 128:(jo + 1) * 128], cout[0:32, (jo % 2) * 128:(jo % 2 + 1) * 128])

        def GATE_CHAIN(j):
            w0c = cst[:, cw + j:cw + j + 1]
            w1c = cst[:, cw + NFF + j:cw + NFF + j + 1]
            w2c = cst[:, cw + 2 * NFF + j:cw + 2 * NFF + j + 1]
            bc = cst[:, cb_ + j:cb_ + j + 1]
            zc, c_, u_ = GST.pop(j)
            if prompt:
                zp = zpb[:, (j % 2) * 4:(j % 2) * 4 + 4]
                cp(zp[:, 0:2], czc[:, j, :])
                stt(c_[:, 1:NT], zc[:, 0:NT - 1], w1c, c_[:, 1:NT], ALU.mult, ALU.add)
                stt(c_[:, 0:1], zp[:, 1:2], w1c, c_[:, 0:1], ALU.mult, ALU.add)
                stt(c_[:, 2:NT], zc[:, 0:NT - 2], w0c, c_[:, 2:NT], ALU.mult, ALU.add)
                stt(c_[:, 0:2], zp[:, 0:2], w0c, c_[:, 0:2], ALU.mult, ALU.add)
                cp(czc[:, j, :], zc[:, NT - 2:NT])
                if last_p:
                    CONV_OUT(j - 2)
                    if j == NFF - 1:
                        CONV_OUT(j - 1)
                        CONV_OUT(j)
            else:
                if j % 4 == 0:
                    stT_cache["cur"] = sample_state(j // 4)
                stT = stT_cache["cur"]
                jj = j % 4
                zall = zc[:, 0:NSEQ * W2].rearrange("p (b t) -> p b t", b=NSEQ)
                cp(zall[:, :, 0:2], stT[:, jj * 32:(jj + 1) * 32].rearrange("p (b t) -> p b t", b=NSEQ))
                cv = c_[:, 0:NT].rearrange("p (b t) -> p b t", b=NSEQ)
                stt(cv, zall[:, :, 1:W2 - 1], w1c, cv, ALU.mult, ALU.add)
                stt(cv, zall[:, :, 0:W2 - 2], w0c, cv, ALU.mult, ALU.add)
                cp(cstg[:, (j % 4) * 32:(j % 4 + 1) * 32].rearrange("p (b t) -> p b t", b=NSEQ), zall[:, :, SL:SL + 2])
                CONV_OUT(j - 2)
                if j == NFF - 1:
                    CONV_OUT(j - 1)
                    CONV_OUT(j)
                if jj == 3:
                    free([stT])
            act(u_[:, 0:NT], c_[:, 0:NT], AF.Square)
            stt(u_[:, 0:NT], u_[:, 0:NT], 22.363860002236387, c_[:, 0:NT], ALU.add, ALU.mult)
            act(u_[:, 0:NT], u_[:, 0:NT], AF.Sigmoid, scale=1.5957691216057308 * 0.044715)
            tt(c_[:, 0:NT], c_[:, 0:NT], u_[:, 0:NT], ALU.mult)
            free([zc, u_])
            GCH[j] = c_

        def VAL(j):
            c_ = GCH.pop(j)
            fb = fbuf[(j // GS) % 2][j % GS]
            gemm_fm(("uv", j), hn, NT, lambda ps: tt(fb[:, 0:NT], ps, c_[:, 0:NT], ALU.mult))
            free([c_])

        def DOWN(g):
            fg = fbuf[g % 2]
            for n in range(16):
                if g == 0:
                    gemm_fm(("dn", g, n), fg, NT, lambda ps, n=n: cp(yacc[n][:, 0:NT], ps, eng="act"))
                else:
                    gemm_fm(("dn", g, n), fg, NT, lambda ps, n=n: tt(yacc[n][:, 0:NT], yacc[n][:, 0:NT], ps, ALU.add))

        GATE_MM(0)
        GATE_CHAIN(0)
        pending_down = None
        for j in range(NFF):
            if j + 1 < NFF:
                GATE_MM(j + 1)
            VAL(j)
            if j + 1 < NFF:
                GATE_CHAIN(j + 1)
            if pending_down is not None and (j % GS) == 2:
                DOWN(pending_down)
                pending_down = None
            if (j + 1) % GS == 0:
                pending_down = j // GS
        DOWN(pending_down)
        rfree(fbuf[0] + fbuf[1])
        rfree(hn)
        rs = rms_rstd(yacc, NT)
        for n in range(16):
            stt(yacc[n][:, 0:NT], yacc[n][:, 0:NT], C("g_post_ffn", n), rs[:, 0:NT], ALU.mult, ALU.mult)
            tt(yacc[n][:, 0:NT], yacc[n][:, 0:NT], hT[n][:, 0:NT], ALU.add, eng="dve")
        free([rs])
        free(hT)
        yoff = (ti - 2) if n_ptiles == 4 else ti
        ydst = yp[yoff * TT:(yoff + 1) * TT, :] if prompt else ys
        for blk in range(NBK):
            for qd in range(4):
                pb_ = banks[2 + qd % 2]
                for cc in range(4):
                    n = qd * 4 + cc
                    tr(pb_[:, cc * 128:(cc + 1) * 128], yacc[n][:, blk * 128:(blk + 1) * 128])
                yo = alloc(1)[0]
                cp(yo[:], pb_[:], eng=("act" if qd % 2 else "dve"))
                dma_out(ydst[blk * 128:(blk + 1) * 128, qd * 512:(qd + 1) * 512], yo[:])
                free([yo])
        free(yacc)

    if n_ptiles > 0:
        mem_kv_stage()
    if USE_SCR and PROLOGUE:
        for key in list(L.idx.keys()):
            if key[0] == "mkv":
                continue
            off, n = L.idx[key]
            s_ = wslots[wctr[0] % NSLOT]
            wh_ = whs[wctr[0] % NSLOT]
            wctr[0] += 1
            P.dma("pool", lambda e, s_=s_, off=off, n=n: e.dma_start(out=s_.t[:, 0:n], in_=wall[:, off:off + n]), (), [s_])
            o_ = P.dma("sp", lambda e, s_=s_, off=off, n=n: e.dma_start(out=wscr[:, off:off + n], in_=s_.t[:, 0:n]), [s_], (),
                       sem_buf=wh_)
            scr_ev[key] = o_.ev
    modes = ["rwA", "mixB", "full", "full"]
    for ti in range(n_ptiles):
        tile_stage("p", ti, modes[ti] if n_ptiles == 4 else "full")
    if do_sample:
        tile_stage("s", 0)

    P.emit()
    es.close()
    return nc


_CACHE = {}


def _prep_inputs(inp):
    L, wall = build_wall(inp)
    consts = build_consts(inp)
    in_maps = []
    for c in range(8):
        pb = c % 4
        half = c // 4
        bs = slice(16 * c, 16 * (c + 1))
        if half == 0:
            xpc = np.concatenate([np.zeros((SEQ // 2, D), np.float32), inp["x_prompt"][pb][:SEQ // 2]], axis=0)
        else:
            xpc = inp["x_prompt"][pb]
        m = {
            "xp": np.ascontiguousarray(xpc),
            "flag": np.full((128, 1), float(half), np.float32),
            "xs": np.ascontiguousarray(inp["x_sample"][bs].reshape(128, D)),
            "memp": np.ascontiguousarray(inp["mem_prompt"][pb]),
            "csk": np.ascontiguousarray(inp["cache_swa_k"][bs].reshape(16, 128, 256)),
            "csv": np.ascontiguousarray(inp["cache_swa_v"][bs].reshape(16, 128, 256)),
            "cmk": np.ascontiguousarray(inp["cache_mem_k"][bs].reshape(16, 256, 512)),
            "cmv": np.ascontiguousarray(inp["cache_mem_v"][bs].reshape(16, 256, 512)),
            "sshift": np.ascontiguousarray(inp["state_rwkv_shift"][bs]),
            "swkv": np.ascontiguousarray(inp["state_rwkv_wkv"][bs].reshape(16, 6, 128, 64)),
            "sconv": np.ascontiguousarray(inp["state_ffn_conv"][bs].reshape(32, DFF)),
            "wall": wall,
            "consts": consts,
        }
        in_maps.append(m)
    return L, wall.shape[1], consts.shape[1], in_maps


def kernel(**inp):
    inp = {k: np.asarray(v, np.float32) for k, v in inp.items()}
    L, ncw, ncc, in_maps = _prep_inputs(inp)
    key = (ncw, ncc)
    if key not in _CACHE:
        _CACHE[key] = build_program(L, ncw, ncc)
    nc = _CACHE[key]
    res = run_bass_kernel_spmd(nc, in_maps, core_ids=list(range(8)))
    R = res.results
    y_p = np.stack([np.concatenate([R[b]["yp"], R[b + 4]["yp"]], axis=0) for b in range(4)])
    y_s = np.concatenate([R[c]["ys"].reshape(16, 8, D) for c in range(8)])
    swk_p = np.stack([R[b + 4]["o_swk_p"].reshape(128, 4, 64) for b in range(4)])
    swv_p = np.stack([R[b + 4]["o_swv_p"].reshape(128, 4, 64) for b in range(4)])
    mk_p = np.stack([R[b]["o_mk_p"].reshape(256, 4, 128) for b in range(4)])
    mv_p = np.stack([R[b]["o_mv_p"].reshape(256, 4, 128) for b in range(4)])
    sh_p = np.stack([R[b + 4]["o_shift_p"].reshape(RWP) for b in range(4)])
    wkv_p = np.stack([R[b + 4]["o_wkv_p"].reshape(12, 64, 64) for b in range(4)])
    cv_p = np.stack([R[b + 4]["o_conv_p"] for b in range(4)])
    swk_s = np.concatenate([R[c]["o_swk_s"].reshape(16, 128, 4, 64) for c in range(8)])
    swv_s = np.concatenate([R[c]["o_swv_s"].reshape(16, 128, 4, 64) for c in range(8)])
    sh_s = np.concatenate([R[c]["o_shift_s"] for c in range(8)])
    wkv_s = np.concatenate([R[c]["o_wkv_s"].reshape(16, 12, 64, 64) for c in range(8)])
    cv_s = np.concatenate([R[c]["o_conv_s"].reshape(16, 2, DFF) for c in range(8)])
    return (y_p, y_s, swk_p, swv_p, mk_p, mv_p, sh_p, wkv_p, cv_p, swk_s, swv_s, sh_s, wkv_s, cv_s)
```
